# Optimizing a Trainium2 kernel written in Bass

```python
import functools
import jax, jax.numpy as jnp
from jax import lax
import numpy as np


D_MODEL = 1024
BATCH = 16
SEQ = 256
DEPTH = 4
DEC_BATCH = 8
DEC_SEQ = 4096
PAST_LEN = 256

GRID_W = 64
N_ATT_HEADS = 8
HEAD_DIM = 64
ATT_WIDTH = N_ATT_HEADS * HEAD_DIM
ATT_SCALE = HEAD_DIM ** -0.5
WIN_ROWS = 8
WIN_COLS = 16
LRU_WIDTH = D_MODEL // 2
LRU_BLOCKS = 8
LRU_BLOCK = LRU_WIDTH // LRU_BLOCKS
CONV_W = 4
CONV_LEFT = 2
LRU_C = 8.0
MIX_WIDTH = ATT_WIDTH + LRU_WIDTH
IN_COLS = 3 * ATT_WIDTH + 2 * LRU_WIDTH
D_FF = -(-8 * D_MODEL // (3 * 256)) * 256
EPS = 1e-6
NEG_INF = -1e30

kernel_name = 'hybrid_natten_rglru_diffusion_step'


def rmsnorm(x, g):
    xf = x.astype(jnp.float32)
    y = xf * lax.rsqrt(jnp.mean(xf * xf, axis=-1, keepdims=True) + EPS)
    return (y * g.astype(jnp.float32)).astype(x.dtype)


def adaln(cvec, w_mod, b_mod):
    m = jax.nn.silu(cvec) @ w_mod + b_mod
    return [t[:, None, :] for t in jnp.split(m, 6, axis=-1)]


def split_heads(t):
    return t.reshape(t.shape[0], t.shape[1], N_ATT_HEADS, HEAD_DIM)


def context_attention(q, k, v):
    s = jnp.einsum('bqhd,bkhd->bhqk', q, k).astype(jnp.float32) * ATT_SCALE
    p = jax.nn.softmax(s, axis=-1).astype(v.dtype)
    o = jnp.einsum('bhqk,bkhd->bqhd', p, v)
    return o.reshape(o.shape[0], o.shape[1], ATT_WIDTH)


def latent_attention(q, k, v, k_ctx, v_ctx, rpb):
    bsz, t_len = q.shape[0], q.shape[1]
    rows = t_len // GRID_W
    kh = min(WIN_ROWS, rows)
    n_loc = kh * GRID_W
    qg = q.reshape(bsz, rows, GRID_W, N_ATT_HEADS, HEAD_DIM)
    kg = k.reshape(bsz, rows, GRID_W, N_ATT_HEADS, HEAD_DIM)
    vg = v.reshape(bsz, rows, GRID_W, N_ATT_HEADS, HEAD_DIM)
    col = jnp.arange(GRID_W)
    cs = jnp.clip(col - WIN_COLS // 2, 0, GRID_W - WIN_COLS)
    col_ok = (col[None, :] >= cs[:, None]) & (col[None, :] < cs[:, None] + WIN_COLS)
    col_idx = jnp.clip(col[None, :] - col[:, None] + WIN_COLS - 1, 0, 2 * WIN_COLS - 2)
    key_ok = jnp.tile(col_ok, (1, kh))

    def row_block(r):
        rs = jnp.clip(r - kh // 2, 0, rows - kh)
        kb = lax.dynamic_slice_in_dim(kg, rs, kh, axis=1).reshape(bsz, n_loc, N_ATT_HEADS, HEAD_DIM)
        vb = lax.dynamic_slice_in_dim(vg, rs, kh, axis=1).reshape(bsz, n_loc, N_ATT_HEADS, HEAD_DIM)
        qr = lax.dynamic_index_in_dim(qg, r, axis=1, keepdims=False)
        row_idx = rs + jnp.arange(kh) - r + WIN_ROWS - 1
        bias = rpb[:, row_idx[:, None, None], col_idx[None, :, :]]
        bias = bias.transpose(0, 2, 1, 3).reshape(N_ATT_HEADS, GRID_W, n_loc).astype(jnp.float32)
        s_loc = jnp.einsum('bqhd,bkhd->bhqk', qr, kb).astype(jnp.float32) * ATT_SCALE + bias
        s_loc = jnp.where(key_ok, s_loc, NEG_INF)
        s_ctx = jnp.einsum('bqhd,bkhd->bhqk', qr, k_ctx).astype(jnp.float32) * ATT_SCALE
        p = jax.nn.softmax(jnp.concatenate([s_loc, s_ctx], axis=-1), axis=-1).astype(v.dtype)
        return (jnp.einsum('bhqk,bkhd->bqhd', p[..., :n_loc], vb)
                + jnp.einsum('bhqk,bkhd->bqhd', p[..., n_loc:], v_ctx))

    o = lax.map(row_block, jnp.arange(rows))
    return o.transpose(1, 0, 2, 3, 4).reshape(bsz, t_len, ATT_WIDTH)


def centred_conv(x, w, b):
    t_len = x.shape[1]
    xp = jnp.pad(x, ((0, 0), (CONV_LEFT, CONV_W - 1 - CONV_LEFT), (0, 0)))
    y = b
    for j in range(CONV_W):
        y = y + w[j] * xp[:, j:j + t_len]
    return y


def block_diag(x, w, b):
    xb = x.reshape(x.shape[0], x.shape[1], LRU_BLOCKS, LRU_BLOCK)
    return jnp.einsum('btnj,njk->btnk', xb, w).reshape(x.shape) + b


def _lru_combine(e1, e2):
    a1, b1 = e1
    a2, b2 = e2
    return a1 * a2, a2 * b1 + b2


def rg_lru(xc, a_param, w_r, b_r, w_i, b_i, h0, reverse):
    r = jax.nn.sigmoid(block_diag(xc, w_r, b_r).astype(jnp.float32))
    i = jax.nn.sigmoid(block_diag(xc, w_i, b_i).astype(jnp.float32))
    log_a = LRU_C * r * jax.nn.log_sigmoid(a_param.astype(jnp.float32))
    a = jnp.exp(log_a)
    bx = jnp.sqrt(-jnp.expm1(2.0 * log_a)) * i * xc.astype(jnp.float32)
    edge = -1 if reverse else 0
    bx = bx.at[:, edge].add(a[:, edge] * h0.astype(jnp.float32))
    _, h = lax.associative_scan(_lru_combine, (a, bx), reverse=reverse, axis=1)
    return h


def lru_branch(xb, yb, conv_w, conv_b, lru_a, lru_wr, lru_br, lru_wi, lru_bi, h0):
    xc = centred_conv(xb, conv_w, conv_b)
    hf = rg_lru(xc, lru_a[0], lru_wr[0], lru_br[0], lru_wi[0], lru_bi[0], h0[:, 0], False)
    hb = rg_lru(xc, lru_a[1], lru_wr[1], lru_br[1], lru_wi[1], lru_bi[1], h0[:, 1], True)
    out = (hf + hb).astype(yb.dtype) * jax.nn.gelu(yb)
    final = jnp.stack([hf[:, -1], hb[:, 0]], axis=1)
    return out, final


def trunk_layer(x, mod, attend, h0, norm1, norm2, w_in, w_out, conv_w, conv_b,
                lru_a, lru_wr, lru_br, lru_wi, lru_bi, w_gate, w_up, w_down):
    sh1, sc1, g1, sh2, sc2, g2 = mod
    h = rmsnorm(x, norm1) * (1 + sc1) + sh1
    p = h @ w_in
    q = split_heads(p[..., :ATT_WIDTH])
    k = split_heads(p[..., ATT_WIDTH:2 * ATT_WIDTH])
    v = split_heads(p[..., 2 * ATT_WIDTH:3 * ATT_WIDTH])
    xb = p[..., 3 * ATT_WIDTH:3 * ATT_WIDTH + LRU_WIDTH]
    yb = p[..., 3 * ATT_WIDTH + LRU_WIDTH:]
    att = attend(q, k, v)
    rec, h_final = lru_branch(xb, yb, conv_w, conv_b, lru_a, lru_wr, lru_br, lru_wi, lru_bi, h0)
    x = x + g1 * (jnp.concatenate([att, rec], axis=-1) @ w_out)
    h2 = rmsnorm(x, norm2) * (1 + sc2) + sh2
    x = x + g2 * ((jax.nn.silu(h2 @ w_gate) * (h2 @ w_up)) @ w_down)
    return x, k, v, h_final


def setup_inputs(seed: int = 0) -> dict:
    key = jax.random.key(seed)
    ks = jax.random.split(key, 32)

    def nrm(k, shape, scale):
        return jax.random.normal(k, shape, jnp.float32) * scale

    u = jax.random.uniform(ks[16], (DEPTH, 2, LRU_WIDTH), jnp.float32, 0.9, 0.999)
    return {
        'x_prompt': nrm(ks[0], (BATCH, SEQ, D_MODEL), 1.0),
        'x_sample': nrm(ks[1], (DEC_BATCH, DEC_SEQ, D_MODEL), 1.0),
        'cache_k': nrm(ks[2], (DEC_BATCH, DEPTH, PAST_LEN, N_ATT_HEADS, HEAD_DIM), 1.0),
        'cache_v': nrm(ks[3], (DEC_BATCH, DEPTH, PAST_LEN, N_ATT_HEADS, HEAD_DIM), 1.0),
        'state_lru': nrm(ks[4], (DEC_BATCH, DEPTH, 2, LRU_WIDTH), 0.5),
        'c': nrm(ks[5], (DEC_BATCH, D_MODEL), 1.0),
        'c_ctx': nrm(ks[6], (D_MODEL,), 1.0),
        'w_mod': nrm(ks[7], (DEPTH, D_MODEL, 6 * D_MODEL), 0.5 * D_MODEL ** -0.5),
        'b_mod': nrm(ks[8], (DEPTH, 6 * D_MODEL), 0.02),
        'norm1': 1.0 + nrm(ks[9], (DEPTH, D_MODEL), 0.02),
        'norm2': 1.0 + nrm(ks[10], (DEPTH, D_MODEL), 0.02),
        'w_in': nrm(ks[11], (DEPTH, D_MODEL, IN_COLS), D_MODEL ** -0.5),
        'w_out': nrm(ks[12], (DEPTH, MIX_WIDTH, D_MODEL), MIX_WIDTH ** -0.5),
        'rpb': nrm(ks[13], (DEPTH, N_ATT_HEADS, 2 * WIN_ROWS - 1, 2 * WIN_COLS - 1), 0.1),
        'conv_w': nrm(ks[14], (DEPTH, CONV_W, LRU_WIDTH), CONV_W ** -0.5),
        'conv_b': nrm(ks[15], (DEPTH, LRU_WIDTH), 0.02),
        'lru_a': jnp.log(u) - jnp.log1p(-u),
        'lru_wr': nrm(ks[17], (DEPTH, 2, LRU_BLOCKS, LRU_BLOCK, LRU_BLOCK), LRU_BLOCK ** -0.5),
        'lru_br': nrm(ks[18], (DEPTH, 2, LRU_WIDTH), 0.02),
        'lru_wi': nrm(ks[19], (DEPTH, 2, LRU_BLOCKS, LRU_BLOCK, LRU_BLOCK), LRU_BLOCK ** -0.5),
        'lru_bi': nrm(ks[20], (DEPTH, 2, LRU_WIDTH), 0.02),
        'w_gate': nrm(ks[21], (DEPTH, D_MODEL, D_FF), D_MODEL ** -0.5),
        'w_up': nrm(ks[22], (DEPTH, D_MODEL, D_FF), D_MODEL ** -0.5),
        'w_down': nrm(ks[23], (DEPTH, D_FF, D_MODEL), D_FF ** -0.5),
        'norm_final': 1.0 + nrm(ks[24], (D_MODEL,), 0.02),
    }


def reference(x_prompt, x_sample, cache_k, cache_v, state_lru, c, c_ctx, w_mod, b_mod, norm1, norm2,
              w_in, w_out, rpb, conv_w, conv_b, lru_a, lru_wr, lru_br, lru_wi, lru_bi,
              w_gate, w_up, w_down, norm_final):
    xp, xs = x_prompt, x_sample
    h0_ctx = jnp.zeros((x_prompt.shape[0], 2, LRU_WIDTH), jnp.float32)
    ks_out, vs_out, hs_out = [], [], []
    for l in range(DEPTH):
        weights = (norm1[l], norm2[l], w_in[l], w_out[l], conv_w[l], conv_b[l], lru_a[l], lru_wr[l],
                   lru_br[l], lru_wi[l], lru_bi[l], w_gate[l], w_up[l], w_down[l])
        mod_ctx = adaln(c_ctx[None, :], w_mod[l], b_mod[l])
        xp, k_l, v_l, h_l = trunk_layer(xp, mod_ctx, context_attention, h0_ctx, *weights)
        ks_out.append(k_l)
        vs_out.append(v_l)
        hs_out.append(h_l)
        mod_lat = adaln(c, w_mod[l], b_mod[l])
        attend_lat = functools.partial(latent_attention, k_ctx=cache_k[:, l], v_ctx=cache_v[:, l], rpb=rpb[l])
        xs, _, _, _ = trunk_layer(xs, mod_lat, attend_lat, state_lru[:, l], *weights)
    y_prompt = rmsnorm(xp, norm_final)
    y_sample = rmsnorm(xs, norm_final)
    new_cache_k = jnp.stack(ks_out, axis=1)
    new_cache_v = jnp.stack(vs_out, axis=1)
    new_state_lru = jnp.stack(hs_out, axis=1)
    return (y_prompt, y_sample, new_cache_k, new_cache_v, new_state_lru)
```

```python
import numpy as np
from contextlib import ExitStack
import concourse.bass as bass
import concourse.mybir as mybir
from concourse.bass_utils import run_bass_kernel_spmd

F32 = mybir.dt.float32
BF16 = mybir.dt.bfloat16
ALU = mybir.AluOpType
AF = mybir.ActivationFunctionType

D = 1024
L = 4
NH = 8
HD = 64
DFF = 2816
NFF = DFF // 128
SEQ_S = 4096
SEQ_P = 256
TOK = SEQ_S + 2 * SEQ_P
TS = 512
NT = TOK // TS
INC = 2560
EPS = 1e-6
NEG = -30000.0
NW = 22


class Op:
    __slots__ = ("eng", "fn", "dma", "chan", "deps", "signals", "sigval")


class Sched:
    COMPUTE = ("pe", "act", "dve", "pool")

    def __init__(self):
        self.ops = []
        self.last_writer = {}
        self.readers = {}
        self.barrier_ops = []
        self.last_compute = {}
        self.last_dma = {}

    def add(self, eng, fn, reads=(), writes=(), dma=False, chan=None):
        op = Op()
        op.eng, op.fn, op.dma, op.chan = eng, fn, dma, chan
        op.signals = dma
        op.sigval = 0
        deps = {}
        for b in self.barrier_ops:
            deps[b] = deps.get(b, False)
        for k in reads:
            w = self.last_writer.get(k)
            if w is not None:
                deps[w] = True
            if isinstance(k, tuple) and k[0] == "ps":
                for r in self.readers.get(k, ()):
                    if r.eng != eng:
                        deps[r] = deps.get(r, False)
        for k in writes:
            w = self.last_writer.get(k)
            if w is not None:
                deps[w] = deps.get(w, False)
            for r in self.readers.get(k, ()):
                deps[r] = deps.get(r, False)
        keep = []
        for d, raw in deps.items():
            if d.dma or dma or d.eng != eng:
                keep.append(d)
            elif eng == "pool" or (raw and eng != "pe"):
                keep.append(d)
        op.deps = keep
        for d in keep:
            d.signals = True
        for k in reads:
            self.readers.setdefault(k, []).append(op)
        for k in writes:
            self.last_writer[k] = op
            self.readers[k] = []
        if dma:
            assert chan is not None
            self.last_dma[chan] = op
        else:
            self.last_compute[eng] = op
        self.ops.append(op)
        return op

    def barrier(self):
        self.barrier_ops = list(self.last_compute.values()) + list(self.last_dma.values())
        for b in self.barrier_ops:
            b.signals = True

    def emit(self, nc, es):
        chans = sorted({op.chan for op in self.ops if op.dma})
        sems = {}
        for e in self.COMPUTE:
            sems[("eng", e)] = es.enter_context(nc.semaphore("s_" + e))
        for c in chans:
            sems[("chan", c)] = es.enter_context(nc.semaphore("c_" + c))
        cnt = {}
        for op in self.ops:
            key = ("chan", op.chan) if op.dma else ("eng", op.eng)
            if op.dma:
                cnt[key] = cnt.get(key, 0) + 16
                op.sigval = cnt[key]
            elif op.signals:
                cnt[key] = cnt.get(key, 0) + 1
                op.sigval = cnt[key]
        final = dict(cnt)
        streams = {e: [] for e in ("pe", "act", "dve", "pool", "sp")}
        for op in self.ops:
            streams[op.eng].append(op)

        def run(name, e):
            waited = {}
            for op in streams[name]:
                need = {}
                for d in op.deps:
                    key = ("chan", d.chan) if d.dma else ("eng", d.eng)
                    if d.sigval > need.get(key, 0):
                        need[key] = d.sigval
                for key, v in need.items():
                    if waited.get(key, 0) < v:
                        e.wait_ge(sems[key], v)
                        waited[key] = v
                ins = op.fn(e)
                if op.dma:
                    ins.then_inc(sems[("chan", op.chan)], 16)
                elif op.signals:
                    ins.then_inc(sems[("eng", op.eng)], 1)
            if name == "sp":
                for key, v in final.items():
                    e.wait_ge(sems[key], v)

        with nc.Block() as block:
            @block.tensor
            def _(e):
                run("pe", e)

            @block.scalar
            def _(e):
                run("act", e)

            @block.vector
            def _(e):
                run("dve", e)

            @block.gpsimd
            def _(e):
                run("pool", e)

            @block.sync
            def _(e):
                run("sp", e)
        return len(chans) + 4


class ColVecs:
    def __init__(self):
        self.cols = {}
        self.n = 0

    def reg(self, name, k):
        self.cols[name] = self.n
        self.n += k

    def __getitem__(self, name):
        return self.cols[name]


def make_layout():
    cvl = ColVecs()
    for l in range(L):
        cvl.reg(("norm1", l), 8)
        cvl.reg(("norm2", l), 8)
        cvl.reg(("bmod", l), 48)
        for j in range(4):
            cvl.reg(("convw", l, j), 4)
        cvl.reg(("convb", l), 4)
    cvl.reg("lrua", L * 2 * 4)
    for l in range(L):
        for d in range(2):
            cvl.reg(("br", l, d), 4)
            cvl.reg(("bi", l, d), 4)
            cvl.reg(("h0", l, d), 4)
    cvl.reg("normf", 8)
    cvl.reg("c", 8)
    cvl.reg("cctx", 8)
    return cvl


CVL = make_layout()
NV = CVL.n


def build_consts():
    ident = np.eye(128, dtype=np.float32)
    kc = np.arange(64)[:, None]
    qc = np.arange(64)[None, :]
    cs = np.clip(qc - 8, 0, 48)
    colok = (kc >= cs) & (kc < cs + 16)
    CI = np.full((128, NW, 64), NEG, np.float32)
    CF = np.full((128, NW, 64), NEG, np.float32)
    for a in range(2):
        for w in range(NW):
            ri = 17 - w + a
            if 0 <= ri <= 14:
                CF[a * 64:(a + 1) * 64, w, :] = np.where(colok, 0.0, NEG)
            if 3 <= ri <= 10:
                CI[a * 64:(a + 1) * 64, w, :] = np.where(colok, 0.0, NEG)

    def edge(qr0, kr0):
        E = np.full((128, 6, 8, 64), NEG, np.float32)
        for t in range(6):
            for a in range(2):
                kr = kr0 + 2 * t + a
                for b in range(8):
                    qr = qr0 + b
                    rs = min(max(qr - 4, 0), 56)
                    if rs <= kr < rs + 8:
                        E[a * 64:(a + 1) * 64, t, b, :] = 0.0
        return E.reshape(128, 6, 512)

    return ident, CI, CF, edge(0, 0), edge(56, 52)


def build_program(nlayers=L, debug=False, stop_after=None):
    nc = bass.Bass("TRN2", target_bir_lowering=False)
    S = Sched()
    es = ExitStack()

    def din(name, shape, dt=F32):
        return nc.dram_tensor(name, list(shape), dt, kind="ExternalInput").ap()

    def dout(name, shape, dt=F32):
        return nc.dram_tensor(name, list(shape), dt, kind="ExternalOutput").ap()

    def dscr(name, shape, dt=F32):
        kind = "ExternalOutput" if debug else "Internal"
        return nc.dram_tensor(name, list(shape), dt, kind=kind).ap()

    xs_d = din("xs", [SEQ_S, D])
    xp_d = din("xp", [2 * SEQ_P, D])
    ck_d = din("ck", [L, 256, 512])
    cvv_d = din("cvv", [L, 256, 512])
    cv_d = din("cv", [128, NV])
    wmod_d = din("w_mod", [nlayers, D, 6 * D])
    win_d = din("w_in", [nlayers, D, INC])
    wout_d = din("w_out", [nlayers, D, D])
    wg_d = din("w_gate", [nlayers, D, DFF])
    wu_d = din("w_up", [nlayers, D, DFF])
    wd_d = din("w_down", [nlayers, DFF, D])
    rpb_d = din("rpadh", [nlayers, NH * 15, 127])
    wr_d = din("lru_wr", [nlayers, 2, 8, 64, 64])
    wi_d = din("lru_wi", [nlayers, 2, 8, 64, 64])
    ident_d = din("ident", [128, 128])
    CI_d = din("CI", [128, NW, 64])
    CF_d = din("CF", [128, NW, 64])
    E0_d = din("E0", [128, 6, 512], BF16)
    E7_d = din("E7", [128, 6, 512], BF16)

    ys_d = dout("ys", [SEQ_S, D])
    yp_d = dout("yp", [2 * SEQ_P, D])
    nk_d = dout("nk", [2, L, SEQ_P, 512])
    nv_d = dout("nv", [2, L, SEQ_P, 512])
    ns_d = dout("ns", [2, L, 2, 512])

    xT_d = dscr("xT", [8, 128, TOK])
    qT_d = dscr("qT", [4, 128, TOK], BF16)
    kT_d = dscr("kT", [4, 128, TOK], BF16)
    vt_d = dscr("vtok", [TOK, 1024], BF16)
    xbT_d = dscr("xbT", [4, 128, TOK])
    gyT_d = dscr("gyT", [4, 128, TOK])
    xcT_d = dscr("xcT", [4, 128, TOK])
    hfT_d = dscr("hfT", [4, 128, TOK])
    mixT_d = dscr("mixT", [8, 128, TOK], BF16)
    toe_d = dscr("toe", [NH * 15, 64, 64])

    xT_v = xT_d.rearrange("c p t -> p c t")
    qT_v = qT_d.rearrange("c p t -> p c t")
    kT_v = kT_d.rearrange("c p t -> p c t")
    xbT_v = xbT_d.rearrange("c p t -> p c t")
    gyT_v = gyT_d.rearrange("c p t -> p c t")
    xcT_v = xcT_d.rearrange("c p t -> p c t")
    hfT_v = hfT_d.rearrange("c p t -> p c t")
    mixT_v = mixT_d.rearrange("c p t -> p c t")

    uid = [0]

    def sb(name, shape, dt, stack):
        uid[0] += 1
        return stack.enter_context(nc.sbuf_tensor(f"sb{uid[0]}_{name}", list(shape), dt))

    ps = [es.enter_context(nc.psum_tensor(f"ps{i}", [128, 512], F32)) for i in range(8)]

    def pk(i):
        return ("ps", i)

    ident_sb = sb("ident_sb", [128, 128], F32, es)
    ones_bf = sb("ones_bf", [128, 128], BF16, es)
    cv_sb = sb("cv_sb", [128, NV], F32, es)
    modsb = sb("modsb", [128, L, 2, 48], F32, es)
    gains = sb("gains", [128, L, 2, 2, 8], F32, es)
    la_sb = sb("la_sb", [128, 2, L * 2 * 4], F32, es)

    def cvc(name, i=0):
        c0 = CVL[name] + i
        return cv_sb[:, c0:c0 + 1]

    def dma(eng, out, in_, reads, writes, chan, slow=False):
        if slow:
            S.add(eng, lambda e: e.dma_start(out=out, in_=in_, allow_slow_non_contiguous=True), reads=reads,
                  writes=writes, dma=True, chan=chan)
        else:
            S.add(eng, lambda e: e.dma_start(out=out, in_=in_), reads=reads, writes=writes, dma=True, chan=chan)

    def act(out, in_, func, reads, writes, bias=None, scale=None):
        kw = {}
        if bias is not None:
            kw["bias"] = bias
        if scale is not None:
            kw["scale"] = scale
        S.add("act", lambda e: e.activation(out=out, in_=in_, func=func, **kw), reads=reads, writes=writes)

    def mm(out, lhsT, rhs, start, stop, reads, writes):
        S.add("pe", lambda e: e.matmul(out, lhsT, rhs, start=start, stop=stop), reads=reads, writes=writes)


    def tt(eng, out, in0, in1, op, reads, writes):
        S.add(eng, lambda e: e.tensor_tensor(out, in0, in1, op), reads=reads, writes=writes)

    def stt(out, in0, scalar, in1, op0, op1, reads, writes):
        S.add("dve", lambda e: e.scalar_tensor_tensor(out, in0, scalar, in1, op0, op1), reads=reads, writes=writes)

    def tsc(eng, out, in0, s1, s2, op0, op1, reads, writes):
        if s2 is None:
            S.add(eng, lambda e: e.tensor_scalar(out, in0, s1, None, op0), reads=reads, writes=writes)
        else:
            S.add(eng, lambda e: e.tensor_scalar(out, in0, s1, s2, op0, op1), reads=reads, writes=writes)

    def cp(eng, out, in_, reads, writes):
        S.add(eng, lambda e: e.tensor_copy(out, in_), reads=reads, writes=writes)

    def tr(out, in_, reads, writes):
        S.add("pe", lambda e: e.transpose(out, in_, ident_sb[:]), reads=reads + ["ident"], writes=writes)

    def scan(out, d0, d1, init, reads, writes):
        S.add("dve", lambda e: e.tensor_tensor_scan(out, d0, d1, init, ALU.mult, ALU.add), reads=reads, writes=writes)

    def mset(ap, val, writes):
        S.add("pool", lambda e: e.memset(ap, val), writes=writes)

    dma("sp", ident_sb[:], ident_d, [], ["ident"], "ident")
    dma("sp", cv_sb[:], cv_d, [], ["cv"], "cv")
    mset(ones_bf[:], 1.0, ["ones"])

    with ExitStack() as ph:
        sc_f = sb("sc_f", [128, 2, 8], F32, ph)
        sc_bf = sb("sc_bf", [128, 8, 2], BF16, ph)
        wm_bf = sb("wm_bf", [128, 8, 6 * D], BF16, ph)
        sp_t = sb("sp_t", [128, 32], F32, ph)
        act(sc_f[:, 0, :], cv_sb[:, CVL["c"]:CVL["c"] + 8], AF.Silu, ["cv"], ["sc_f"])
        act(sc_f[:, 1, :], cv_sb[:, CVL["cctx"]:CVL["cctx"] + 8], AF.Silu, ["cv"], ["sc_f"])
        for s in range(2):
            cp("dve", sc_bf[:, :, s], sc_f[:, s, :], ["sc_f"], ["sc_bf"])
        a0 = CVL["lrua"]
        act(sp_t[:], cv_sb[:, a0:a0 + 32], AF.Exp, ["cv"], ["sp_t"], scale=-1.0)
        act(sp_t[:], sp_t[:], AF.Ln, ["sp_t"], ["sp_t"], bias=1.0)
        tsc("dve", la_sb[:, 0, :], sp_t[:], -8.0, None, ALU.mult, None, ["sp_t"], ["la"])
        tsc("dve", la_sb[:, 1, :], sp_t[:], -16.0, None, ALU.mult, None, ["sp_t"], ["la"])
        for l in range(nlayers):
            wsrc = wmod_d[l].rearrange("(kc p) n -> p kc n", p=128)
            for part in range(4):
                cs_ = slice(part * 1536, (part + 1) * 1536)
                dma("pool", wm_bf[:, :, cs_], wsrc[:, :, cs_], [], ["wm"], "wm")
            for oc in range(48):
                for kc in range(8):
                    mm(ps[0][:, oc * 2:oc * 2 + 2], wm_bf[:, kc, oc * 128:(oc + 1) * 128], sc_bf[:, kc, :],
                       kc == 0, kc == 7, ["wm", "sc_bf"], [pk(0)])
            psv = ps[0][:, 0:96].rearrange("p (o s) -> p o s", s=2)
            b0 = CVL[("bmod", l)]
            for s in range(2):
                tt("dve", modsb[:, l, s, :], psv[:, :, s], cv_sb[:, b0:b0 + 48], ALU.add, [pk(0), "cv"], ["modsb"])
            for s in range(2):
                for ni in range(2):
                    sc_off = 8 if ni == 0 else 32
                    n0 = CVL[("norm1", l)] if ni == 0 else CVL[("norm2", l)]
                    stt(gains[:, l, s, ni, :], modsb[:, l, s, sc_off:sc_off + 8], 1.0, cv_sb[:, n0:n0 + 8],
                        ALU.add, ALU.mult, ["modsb", "cv"], ["gains"])
    S.barrier()

    def mod_col(l, s, which, c):
        return modsb[:, l, s, which * 8 + c:which * 8 + c + 1]

    def gain_col(l, s, ni, c):
        return gains[:, l, s, ni, c:c + 1]

    def norm_pre(xt, xkey, sq, sqkey):
        for hf in range(2):
            act(sq[:, hf * 4:(hf + 1) * 4, :], xt[:, hf * 4:(hf + 1) * 4, :], AF.Square, [xkey], [sqkey])

    def norm_mid(sq, sqkey, lnm):
        for c in range(8):
            mm(ps[0][:], ones_bf[:], sq[:, c, :], c == 0, c == 7, ["ones", sqkey], [pk(0)])
        act(lnm[:], ps[0][:], AF.Ln, [pk(0)], ["lnm"], bias=EPS, scale=1.0 / D)
        act(ps[1][:], lnm[:], AF.Exp, ["lnm"], [pk(1)], scale=-0.5)

    def norm_post(xt, xkey, tmp2, tmpkey, out, outkey, gain_fn, shift_fn):
        for c in range(8):
            if shift_fn is None:
                stt(out[:, c, :], xt[:, c, :], gain_fn(c), ps[1][:], ALU.mult, ALU.mult,
                    [xkey, pk(1), "gains", "cv"], [outkey])
            else:
                tk = (tmpkey, c % 2)
                stt(tmp2[:, c % 2, :], xt[:, c, :], gain_fn(c), ps[1][:], ALU.mult, ALU.mult,
                    [xkey, pk(1), "gains"], [tk])
                act(out[:, c, :], tmp2[:, c % 2, :], AF.Identity, [tk, "modsb"], [outkey], bias=shift_fn(c))

    def norm_tile(xt, xkey, sq, sqkey, tmp2, tmpkey, out, outkey, gain_fn, shift_fn, lnm):
        norm_pre(xt, xkey, sq, sqkey)
        norm_mid(sq, sqkey, lnm)
        norm_post(xt, xkey, tmp2, tmpkey, out, outkey, gain_fn, shift_fn)

    with ExitStack() as ph:
        xin = [sb(f"xin{i}", [128, 4, D], F32, ph) for i in range(2)]
        xst = [sb(f"xst{i}", [128, 8, TS], F32, ph) for i in range(2)]

        def p0_load(j):
            sl = j % 2
            if j < 8:
                src = xs_d[j * TS:(j + 1) * TS, :].rearrange("(s p) f -> p s f", p=128)
            else:
                src = xp_d.rearrange("(s p) f -> p s f", p=128)
            dma("sp", xin[sl][:], src, [], [("xin", sl)], f"xin{sl}")

        p0_load(0)
        for j in range(NT):
            sl = j % 2
            if j + 1 < NT:
                p0_load(j + 1)
            for c in range(8):
                for s in range(4):
                    tr(ps[c][:, s * 128:(s + 1) * 128], xin[sl][:, s, c * 128:(c + 1) * 128], [("xin", sl)], [pk(c)])
                if c % 2 == 0:
                    act(xst[sl][:, c, :], ps[c][:], AF.Copy, [pk(c)], [("xst", sl)])
                else:
                    cp("dve", xst[sl][:, c, :], ps[c][:], [pk(c)], [("xst", sl)])
            dma("sp", xT_v[:, :, j * TS:(j + 1) * TS], xst[sl][:], [("xst", sl)], [("xT", j)], f"xst{sl}")
    S.barrier()
    nl_run = 0 if stop_after == "p0" else nlayers

    for l in range(nl_run):
        with ExitStack() as ph:
            w_in = sb("w_in", [128, 8, INC], BF16, ph)
            xt = [sb(f"a_xt{i}", [128, 8, TS], F32, ph) for i in range(2)]
            tmp2 = sb("a_tmp", [128, 2, TS], F32, ph)
            hTs = [sb(f"a_hT{i}", [128, 8, TS], BF16, ph) for i in range(2)]
            lnm = sb("a_lnm", [128, TS], F32, ph)
            qk = [sb(f"a_qk{i}", [128, 8, TS], BF16, ph) for i in range(2)]
            vst = [sb(f"a_vst{i}", [128, 4, 1024], BF16, ph) for i in range(2)]
            for i in range(2):
                mset(vst[i][:], 1.0, [("a_vst", i)])
            xy = [sb(f"a_xy{i}", [128, 8, TS], F32, ph) for i in range(2)]
            kvf = sb("a_kvf", [128, 2, 4, 512], F32, ph)
            wsrc = win_d[l].rearrange("(kc p) n -> p kc n", p=128)
            for part in range(2):
                cs_ = slice(part * 1280, (part + 1) * 1280)
                dma("pool", w_in[:, :, cs_], wsrc[:, :, cs_], [], ["w_in"], "w_in")

            def a_load(j):
                sl = j % 2
                dma("sp", xt[sl][:], xT_v[:, :, j * TS:(j + 1) * TS], [("xT", j)], [("a_xt", sl)], f"a_xt{sl}")

            a_load(0)
            a_load(1)
            bank = [0]

            def nxt():
                b = 2 + bank[0] % 6
                bank[0] += 1
                return b

            def a_norm(j, part):
                sl_ = j % 2
                st_ = 1 if j == 8 else 0
                hk = ("a_hT", sl_)
                if part == 0:
                    norm_pre(xt[sl_], ("a_xt", sl_), hTs[sl_], hk)
                else:
                    norm_mid(hTs[sl_], hk, lnm)
                    norm_post(xt[sl_], ("a_xt", sl_), tmp2, "a_tmp", hTs[sl_], hk,
                              lambda c, st_=st_, l=l: gain_col(l, st_, 0, c), lambda c, st_=st_, l=l: mod_col(l, st_, 0, c))

            a_norm(0, 0)
            a_norm(0, 1)
            for j in range(NT):
                sl = j % 2
                st = 1 if j == 8 else 0
                t0 = j * TS
                hT = hTs[sl]
                hkey = ("a_hT", sl)
                if j + 1 < NT:
                    a_norm(j + 1, 0)
                gi = 0
                for n in list(range(0, 8)) + list(range(12, 20)):
                    if gi == 5 and j + 1 < NT:
                        a_norm(j + 1, 1)
                    gi += 1
                    b = nxt()
                    for kc in range(8):
                        mm(ps[b][:], w_in[:, kc, n * 128:(n + 1) * 128], hT[:, kc, :], kc == 0, kc == 7,
                           ["w_in", hkey], [pk(b)])
                    if n < 4:
                        act(qk[sl][:, n, :], ps[b][:], AF.Copy, [pk(b)], [("a_qk", sl)], scale=0.125)
                    elif n < 8:
                        cp("dve", qk[sl][:, n, :], ps[b][:], [pk(b)], [("a_qk", sl)])
                    elif n < 16:
                        cp("dve", xy[sl][:, n - 12, :], ps[b][:], [pk(b)], [("a_xy", sl)])
                    else:
                        act(xy[sl][:, n - 12, :], ps[b][:], AF.Gelu_apprx_tanh, [pk(b)], [("a_xy", sl)])
                import os as _os
                J8 = (j == 8) and not _os.environ.get('SKIP8')
                for which in ([1, 0] if J8 else [1]):
                    col0 = 1024 if which == 1 else 512
                    for s in range(4):
                        b = nxt()
                        for kc in range(8):
                            mm(ps[b][:], hT[:, kc, s * 128:(s + 1) * 128], w_in[:, kc, col0:col0 + 512],
                               kc == 0, kc == 7, ["w_in", hkey], [pk(b)])
                        if J8:
                            if s % 2 == 1:
                                act(kvf[:, which, s, :], ps[b][:], AF.Copy, [pk(b)], ["a_kvf"])
                            else:
                                cp("dve", kvf[:, which, s, :], ps[b][:], [pk(b)], ["a_kvf"])
                            if which == 1:
                                kv5 = kvf[:, which, s, :].rearrange("p (h2 par d) -> p h2 par d", h2=4, par=2)
                                v5 = vst[sl][:, s, :].rearrange("p (h2 par x d) -> p h2 par x d", h2=4, par=2, x=2)
                                for par in range(2):
                                    cp("pool", v5[:, :, par, par, :], kv5[:, :, par, :], ["a_kvf", ("a_vst", sl)], [("a_vst", sl)])
                        elif which == 1:
                            p5 = ps[b][:].rearrange("p (h2 par d) -> p h2 par d", h2=4, par=2)
                            v5 = vst[sl][:, s, :].rearrange("p (h2 par x d) -> p h2 par x d", h2=4, par=2, x=2)
                            for par in range(2):
                                if (s + par) % 2 == 0:
                                    act(v5[:, :, par, par, :], p5[:, :, par, :], AF.Copy, [pk(b), ("a_vst", sl)], [("a_vst", sl)])
                                else:
                                    cp("dve", v5[:, :, par, par, :], p5[:, :, par, :], [pk(b), ("a_vst", sl)], [("a_vst", sl)])
                dma("sp", qT_v[:, :, t0:t0 + TS], qk[sl][:, 0:4, :], [("a_qk", sl)], [], f"a_qk{sl}")
                dma("sp", kT_v[:, :, t0:t0 + TS], qk[sl][:, 4:8, :], [("a_qk", sl)], [], f"a_qk{sl}")
                dma("sp", vt_d[t0:t0 + TS, :].rearrange("(s p) f -> p s f", p=128), vst[sl][:],
                    [("a_vst", sl)], [], f"a_vst{sl}")
                dma("sp", xbT_v[:, :, t0:t0 + TS], xy[sl][:, 0:4, :], [("a_xy", sl)], [], f"a_xy{sl}")
                dma("sp", gyT_v[:, :, t0:t0 + TS], xy[sl][:, 4:8, :], [("a_xy", sl)], [], f"a_xy{sl}")
                if j + 2 < NT:
                    a_load(j + 2)
                if J8:
                    for s_ in range(2):
                        dma("sp", nk_d[s_, l].rearrange("(h p) f -> p h f", p=128), kvf[:, 0, 2 * s_:2 * s_ + 2, :], ["a_kvf"], [], "a_kvf")
                        dma("sp", nv_d[s_, l].rearrange("(h p) f -> p h f", p=128), kvf[:, 1, 2 * s_:2 * s_ + 2, :], ["a_kvf"], [], "a_kvf")
        S.barrier()
        if stop_after == ("A", l):
            break

        with ExitStack() as ph:
            TI = sb("b_TI", [128, NH, NW, 64], F32, ph)
            TF = sb("b_TF", [128, NH, NW, 64], F32, ph)
            with ExitStack() as ph2:
                Tb = sb("b_Tb", [128, NH, NW, 64], F32, ph2)
                CIs = sb("b_CI", [128, NW, 64], F32, ph2)
                CFs = sb("b_CF", [128, NW, 64], F32, ph2)
                dma("sp", CIs[:], CI_d, [], ["b_CI"], "b_CI")
                dma("sp", CFs[:], CF_d, [], ["b_CF"], "b_CF")
                mset(Tb[:], 0.0, ["b_Tb"])
                src = bass.AP(rpb_d.tensor, l * (NH * 15 * 127) + 63, [[127, NH * 15], [-1, 64], [1, 64]])
                dma("sp", toe_d, src, [], ["toe"], "b_toe")
                for h in range(NH):
                    for a in range(2):
                        dma("sp", Tb[a * 64:(a + 1) * 64, h, 3 + a:18 + a, :],
                            toe_d[h * 15:(h + 1) * 15].rearrange("u k q -> k u q"), ["toe", "b_Tb"], ["b_Tb"], "b_Tb")
                for h in range(NH):
                    tt("dve", TI[:, h], Tb[:, h], CIs[:], ALU.add, ["b_Tb", "b_CI"], ["b_TI"])
                    tt("pool", TF[:, h], Tb[:, h], CFs[:], ALU.add, ["b_Tb", "b_CF"], ["b_TF"])
                S.barrier()
            E0s = sb("b_E0", [128, 6, 512], BF16, ph)
            E7s = sb("b_E7", [128, 6, 512], BF16, ph)
            dma("sp", E0s[:], E0_d, [], ["b_E0"], "b_E0")
            dma("sp", E7s[:], E7_d, [], ["b_E7"], "b_E7")
            kcT = sb("b_kcT", [128, 4, 256], BF16, ph)
            vcw = sb("b_vcw", [128, 2, 1024], BF16, ph)
            with ExitStack() as ph2:
                ckf = sb("b_ckf", [128, 2, 512], F32, ph2)
                cvf = sb("b_cvf", [128, 2, 512], F32, ph2)
                dma("sp", ckf[:], ck_d[l].rearrange("(t p) f -> p t f", p=128), [], ["b_ckf"], "b_ckf")
                dma("sp", cvf[:], cvv_d[l].rearrange("(t p) f -> p t f", p=128), [], ["b_cvf"], "b_cvf")
                mset(vcw[:], 1.0, ["b_vcw"])
                for t in range(2):
                    c5 = cvf[:, t, :].rearrange("p (h2 par d) -> p h2 par d", h2=4, par=2)
                    v5 = vcw[:, t, :].rearrange("p (h2 par x d) -> p h2 par x d", h2=4, par=2, x=2)
                    for par in range(2):
                        cp("dve", v5[:, :, par, par, :], c5[:, :, par, :], ["b_cvf", "b_vcw"], ["b_vcw"])
                for c in range(4):
                    for t in range(2):
                        tr(ps[c][:, t * 128:(t + 1) * 128], ckf[:, t, c * 128:(c + 1) * 128], ["b_ckf"], [pk(c)])
                    cp("dve", kcT[:, c, :], ps[c][:, 0:256], [pk(c)], ["b_kcT"])
                S.barrier()
            qt = [sb(f"b_qt{i}", [128, 4, TS], BF16, ph) for i in range(2)]
            kw = [sb(f"b_kw{i}", [128, 4, 1024], BF16, ph) for i in range(2)]
            vw = [sb(f"b_vw{i}", [128, 8, 1024], BF16, ph) for i in range(2)]
            ST = [sb(f"b_ST{i}", [128, TS], F32, ph) for i in range(3)]
            PT = [sb(f"b_PT{i}", [128, 10, TS], BF16, ph) for i in range(2)]
            lnd = [sb(f"b_lnd{i}", [128, TS], F32, ph) for i in range(1)] * 2
            rden = [sb(f"b_rden{i}", [128, TS], F32, ph) for i in range(1)] * 2
            att = [sb(f"b_att{i}", [128, 4, TS], BF16, ph) for i in range(2)]

            blocks = []
            for j in range(8):
                if j == 0:
                    r_lo, nk_, kind, dl = 0, 6, "E0", 0
                elif j == 7:
                    r_lo, nk_, kind, dl = 52, 6, "E7", -4
                else:
                    r_lo, nk_, kind, dl = 8 * j - 4, 8, "I", -4
                blocks.append(dict(q0=j * TS, n=TS, k0=r_lo * 64, nk=nk_, kind=kind, delta0=dl, ctx=True, tile=j, qoff=0))
            for s in range(2):
                blocks.append(dict(q0=SEQ_S + s * SEQ_P, n=SEQ_P, k0=SEQ_S + s * SEQ_P, nk=2, kind="N", delta0=0,
                                   ctx=False, tile=8, qoff=s * SEQ_P))

            def b_load(bi):
                bl = blocks[bi]
                sl = bi % 2
                n, nk_ = bl["n"], bl["nk"]
                dma("sp", qt[sl][:, :, 0:n], qT_v[:, :, bl["q0"]:bl["q0"] + n], [], [("b_qt", sl)], f"b_qt{sl}")
                dma("sp", kw[sl][:, :, 0:nk_ * 128], kT_v[:, :, bl["k0"]:bl["k0"] + nk_ * 128], [], [("b_kw", sl)],
                    f"b_kw{sl}")
                dma("sp", vw[sl][:, 0:nk_, :],
                    vt_d[bl["k0"]:bl["k0"] + nk_ * 128, :].rearrange("(t p) f -> p t f", p=128),
                    [], [("b_vw", sl)], f"b_vw{sl}")

            sctr = [0]

            def lhs_v(tensor_ap, t, h):
                return tensor_ap[:, t, h * 128:(h + 1) * 128]

            def qrange(bl, t):
                kind = bl["kind"]
                if kind == "N" or t >= bl["nk"]:
                    return 0, bl["n"]
                ok = []
                for b_ in range(8):
                    v = False
                    for a_ in range(2):
                        if kind == "I":
                            v = v or (3 <= 2 * t + 3 + a_ - b_ <= 10)
                        else:
                            qr = b_ if kind == "E0" else 56 + b_
                            kr = (2 * t + a_) if kind == "E0" else (52 + 2 * t + a_)
                            rs = min(max(qr - 4, 0), 56)
                            v = v or (rs <= kr < rs + 8)
                    if v:
                        ok.append(b_)
                return ok[0] * 64, (ok[-1] + 1) * 64

            def do_scores(bi, h, hs):
                bl = blocks[bi]
                sl = bi % 2
                n, nk_ = bl["n"], bl["nk"]
                c, po = h // 2, 64 * (h % 2)
                ntl = nk_ + (2 if bl["ctx"] else 0)
                for t in range(ntl):
                    b = 2 + sctr[0] % 4
                    sti = sctr[0] % 3
                    sctr[0] += 1
                    if t < nk_:
                        lhs = kw[sl][po:po + 64, c, t * 128:(t + 1) * 128]
                        rk = [("b_kw", sl)]
                    else:
                        tc_ = t - nk_
                        lhs = kcT[po:po + 64, c, tc_ * 128:(tc_ + 1) * 128]
                        rk = ["b_kcT"]
                    c0, c1 = qrange(bl, t)
                    mm(ps[b][:, c0:c1], lhs, qt[sl][po:po + 64, c, c0:c1], True, True, rk + [("b_qt", sl)], [pk(b)])
                    if t < nk_ and bl["kind"] != "N":
                        w0 = 10 - (bl["delta0"] + 2 * t)
                        tab = TI if bl["kind"] == "I" else TF
                        tv = tab[:, h, w0 + c0 // 64:w0 + c1 // 64, :].rearrange("p w q -> p (w q)")
                        tt("dve", ST[sti][:, c0:c1], ps[b][:, c0:c1], tv, ALU.add, [pk(b), "b_TI", "b_TF"], [("b_ST", sti)])
                        if bl["kind"] != "I":
                            Es = E0s if bl["kind"] == "E0" else E7s
                            tt("pool", ST[sti][:, c0:c1], ST[sti][:, c0:c1], Es[:, t, c0:c1], ALU.add,
                               [("b_ST", sti), "b_E0", "b_E7"], [("b_ST", sti)])
                        act(PT[hs][:, t, c0:c1], ST[sti][:, c0:c1], AF.Exp, [("b_ST", sti)], [("b_PT", hs)])
                    else:
                        act(PT[hs][:, t, c0:c1], ps[b][:, c0:c1], AF.Exp, [pk(b)], [("b_PT", hs)])

            def do_pv(bi, h, hs):
                bl = blocks[bi]
                sl = bi % 2
                asl = bl["tile"] % 2
                n, nk_ = bl["n"], bl["nk"]
                c, po = h // 2, 64 * (h % 2)
                pd = 64 - po
                ntl = nk_ + (2 if bl["ctx"] else 0)
                b = 6 + hs
                torder = list(range(nk_, ntl)) + list(range(nk_))
                for ti, t in enumerate(torder):
                    if t < nk_:
                        lhs = lhs_v(vw[sl], t, h)
                        rk = [("b_vw", sl)]
                    else:
                        lhs = lhs_v(vcw, t - nk_, h)
                        rk = ["b_vcw"]
                    c0, c1 = qrange(bl, t)
                    mm(ps[b][:, c0:c1], lhs, PT[hs][:, t, c0:c1], ti == 0, ti == ntl - 1, rk + [("b_PT", hs)], [pk(b)])
                act(lnd[hs][po:po + 64, 0:n], ps[b][pd:pd + 64, 0:n], AF.Ln, [pk(b)], ["b_lnd"])
                act(rden[hs][po:po + 64, 0:n], lnd[hs][po:po + 64, 0:n], AF.Exp, ["b_lnd"], ["b_rden"],
                    scale=-1.0)
                qo = bl["qoff"]
                tt("dve", att[asl][po:po + 64, c, qo:qo + n], ps[b][po:po + 64, 0:n], rden[hs][po:po + 64, 0:n],
                   ALU.mult, [pk(b), "b_rden"], [("b_att", asl)])

            b_load(0)
            b_load(1)
            units = [(bi, h) for bi in range(len(blocks)) for h in range(NH)]
            do_scores(units[0][0], units[0][1], 0)
            for ui, (bi, h) in enumerate(units):
                bl = blocks[bi]
                if ui + 1 < len(units):
                    nbi, nh = units[ui + 1]
                    do_scores(nbi, nh, (ui + 1) % 2)
                do_pv(bi, h, ui % 2)
                if h == NH - 1:
                    if bi + 2 < len(blocks):
                        b_load(bi + 2)
                    last_of_tile = (bi + 1 == len(blocks)) or (blocks[bi + 1]["tile"] != bl["tile"])
                    if last_of_tile:
                        tj = bl["tile"]
                        dma("sp", mixT_v[:, 0:4, tj * TS:(tj + 1) * TS], att[tj % 2][:], [("b_att", tj % 2)], [],
                            f"b_att{tj % 2}")
        S.barrier()
        if stop_after == ("B", l):
            break

        with ExitStack() as ph:
            wbd = sb("c_wbd", [128, 2, 2, 4, 128], BF16, ph)
            xbh = [sb(f"c_xbh{i}", [128, 4, TS + 4], F32, ph) for i in range(2)]
            xc = [sb(f"c_xc{i}", [128, 4, TS], F32, ph) for i in range(2)]
            gy = [sb(f"c_gy{i}", [128, 4, TS], F32, ph) for i in range(2)]
            xcb = [sb(f"c_xcb{i}", [128, 4, TS], BF16, ph) for i in range(2)]
            rr = [sb(f"c_r{i}", [128, 4, TS], F32, ph) for i in range(2)]
            ii = [sb(f"c_i{i}", [128, 4, TS], F32, ph) for i in range(2)]
            aa = [sb(f"c_a{i}", [128, 4, TS], F32, ph) for i in range(2)]
            ss = [sb(f"c_s{i}", [128, 4, TS], F32, ph) for i in range(2)]
            hfb = [sb(f"c_hf{i}", [128, 4, TS], F32, ph) for i in range(2)]
            hb = [sb(f"c_hb{i}", [128, 4, TS], F32, ph) for i in range(2)]
            rec = [sb(f"c_rec{i}", [128, 4, TS], BF16, ph) for i in range(2)]
            mset(wbd[:], 0.0, ["c_wbd"])
            for d in range(2):
                for g, wsrc_d in enumerate((wr_d, wi_d)):
                    for par in range(2):
                        src = wsrc_d[l, d, par::2].rearrange("n j k -> j n k")
                        dma("pool", wbd[par * 64:(par + 1) * 64, d, g, :, par * 64:(par + 1) * 64], src,
                            ["c_wbd"], ["c_wbd"], "c_wbd")
            segs = [dict(t0=j * TS, n=TS, left=(j > 0), right=(j < 7), sample=True, idx=j) for j in range(8)]
            segs += [dict(t0=SEQ_S + s * SEQ_P, n=SEQ_P, left=False, right=False, sample=False, idx=s) for s in range(2)]
            gctr = [0]

            def gates(d, xcs, xckey, n, p):
                cp("pool", xcb[p][:, :, 0:n], xcs[:, :, 0:n], [xckey], [("c_xcb", p)])
                for c in range(4):
                    for g in range(2):
                        b = gctr[0] % 8
                        gctr[0] += 1
                        mm(ps[b][:, 0:n], wbd[:, d, g, c, :], xcb[p][:, c, 0:n], True, True, ["c_wbd", ("c_xcb", p)], [pk(b)])
                        dst = rr[p] if g == 0 else ii[p]
                        bname = ("br", l, d) if g == 0 else ("bi", l, d)
                        act(dst[:, c, 0:n], ps[b][:, 0:n], AF.Sigmoid, [pk(b), "cv"], [("c_r", p) if g == 0 else ("c_i", p)],
                            bias=cvc(bname, c))
                lcol = (l * 2 + d) * 4
                for c in range(4):
                    act(aa[p][:, c, 0:n], rr[p][:, c, 0:n], AF.Exp, [("c_r", p), "la"], [("c_a", p)],
                        scale=la_sb[:, 0, lcol + c:lcol + c + 1])
                    act(ss[p][:, c, 0:n], rr[p][:, c, 0:n], AF.Exp, [("c_r", p), "la"], [("c_s", p)],
                        scale=la_sb[:, 1, lcol + c:lcol + c + 1])
                for c in range(4):
                    act(ss[p][:, c, 0:n], ss[p][:, c, 0:n], AF.Sqrt, [("c_s", p)], [("c_s", p)], bias=1.0, scale=-1.0)
                for c in range(4):
                    tt("pool", ii[p][:, c, 0:n], ii[p][:, c, 0:n], xcs[:, c, 0:n], ALU.mult, [("c_i", p), xckey], [("c_i", p)])

            def gates_tail(xcs, xckey, n, p):
                for c in range(4):
                    tt("dve", ss[p][:, c, 0:n], ss[p][:, c, 0:n], ii[p][:, c, 0:n], ALU.mult, [("c_s", p), ("c_i", p)], [("c_s", p)])

            def cf_load(si):
                sg = segs[si]
                sl = si % 2
                n, t0 = sg["n"], sg["t0"]
                lo = 0 if sg["left"] else 2
                hi = n + 3 if sg["right"] else n + 2
                if not sg["left"]:
                    mset(xbh[sl][:, :, 0:2], 0.0, [("c_xbh", sl)])
                if not sg["right"]:
                    mset(xbh[sl][:, :, n + 2:n + 3], 0.0, [("c_xbh", sl)])
                dma("sp", xbh[sl][:, :, lo:hi], xbT_v[:, :, t0 - 2 + lo:t0 - 2 + hi], [("c_xbh", sl)], [("c_xbh", sl)],
                    f"c_xbh{sl}")

            def f_stageA(si):
                sg = segs[si]
                sl = si % 2
                n, t0 = sg["n"], sg["t0"]
                if si + 1 < len(segs):
                    cf_load(si + 1)
                xk = ("c_xc", sl)
                for c in range(4):
                    tsc("dve", xc[sl][:, c, 0:n], xbh[sl][:, c, 0:n], cvc(("convw", l, 0), c), cvc(("convb", l), c),
                        ALU.mult, ALU.add, [("c_xbh", sl), "cv"], [xk])
                for jj in range(1, 4):
                    for c in range(4):
                        stt(xc[sl][:, c, 0:n], xbh[sl][:, c, jj:jj + n], cvc(("convw", l, jj), c), xc[sl][:, c, 0:n],
                            ALU.mult, ALU.add, [("c_xbh", sl), xk, "cv"], [xk])
                dma("sp", xcT_v[:, :, t0:t0 + n], xc[sl][:, :, 0:n], [xk], [], f"c_xc{sl}")
                gates(0, xc[sl], xk, n, sl)

            def f_stageB(si):
                sg = segs[si]
                sl = si % 2
                n, t0 = sg["n"], sg["t0"]
                gates_tail(xc[sl], ("c_xc", sl), n, sl)
                for c in range(4):
                    if sg["sample"] and sg["idx"] > 0:
                        init = hfb[1 - sl][:, c, TS - 1:TS]
                    elif sg["sample"]:
                        init = cvc(("h0", l, 0), c)
                    else:
                        init = 0.0
                    scan(hfb[sl][:, c, 0:n], aa[sl][:, c, 0:n], ss[sl][:, c, 0:n], init,
                         [("c_a", sl), ("c_s", sl), ("c_hf", 1 - sl), "cv"], [("c_hf", sl)])
                dma("sp", hfT_v[:, :, t0:t0 + n], hfb[sl][:, :, 0:n], [("c_hf", sl)], [], f"c_hfo{sl}")
                if not sg["sample"]:
                    dma("sp", ns_d[sg["idx"], l, 0].rearrange("(c p o) -> p c o", p=128, o=1),
                        hfb[sl][:, :, n - 1:n], [("c_hf", sl)], [], f"c_nsf{sl}", slow=True)

            cf_load(0)
            f_stageA(0)
            for si in range(len(segs)):
                if si + 1 < len(segs):
                    f_stageA(si + 1)
                f_stageB(si)
            S.barrier()
            order = list(range(7, -1, -1)) + [8, 9]

            def cb_load(oi):
                sg = segs[order[oi]]
                sl = oi % 2
                n, t0 = sg["n"], sg["t0"]
                dma("sp", xc[sl][:, :, 0:n], xcT_v[:, :, t0:t0 + n], [], [("c_xc", sl)], f"c_xc{sl}")
                dma("sp", gy[sl][:, :, 0:n], gyT_v[:, :, t0:t0 + n], [], [("c_gy", sl)], f"c_gy{sl}")
                dma("sp", hfb[sl][:, :, 0:n], hfT_v[:, :, t0:t0 + n], [], [("c_hf", sl)], f"c_hfi{sl}")

            def b_stageA(oi):
                sg = segs[order[oi]]
                sl = oi % 2
                gates(1, xc[sl], ("c_xc", sl), sg["n"], sl)

            def b_stageB(oi):
                sg = segs[order[oi]]
                sl = oi % 2
                n, t0 = sg["n"], sg["t0"]
                gates_tail(xc[sl], ("c_xc", sl), n, sl)
                for c in range(4):
                    if sg["sample"] and sg["idx"] < 7:
                        init = hb[1 - sl][:, c, 0:1]
                    elif sg["sample"]:
                        init = cvc(("h0", l, 1), c)
                    else:
                        init = 0.0
                    scan(hb[sl][:, c, 0:n][:, ::-1], aa[sl][:, c, 0:n][:, ::-1], ss[sl][:, c, 0:n][:, ::-1], init,
                         [("c_a", sl), ("c_s", sl), ("c_hb", 1 - sl), "cv"], [("c_hb", sl)])
                if not sg["sample"]:
                    dma("sp", ns_d[sg["idx"], l, 1].rearrange("(c p o) -> p c o", p=128, o=1),
                        hb[sl][:, :, 0:1], [("c_hb", sl)], [], f"c_nsb{sl}", slow=True)
                for c in range(4):
                    tt("pool", hfb[sl][:, c, 0:n], hfb[sl][:, c, 0:n], hb[sl][:, c, 0:n], ALU.add,
                       [("c_hf", sl), ("c_hb", sl)], [("c_hf", sl)])
                    tt("dve", rec[sl][:, c, 0:n], hfb[sl][:, c, 0:n], gy[sl][:, c, 0:n], ALU.mult,
                       [("c_hf", sl), ("c_gy", sl)], [("c_rec", sl)])
                dma("sp", mixT_v[:, 4:8, t0:t0 + n], rec[sl][:, :, 0:n], [("c_rec", sl)], [], f"c_rec{sl}")

            cb_load(0)
            cb_load(1)
            b_stageA(0)
            for oi in range(len(order)):
                if oi + 1 < len(order):
                    b_stageA(oi + 1)
                b_stageB(oi)
                if oi + 2 < len(order):
                    cb_load(oi + 2)
        S.barrier()
        if stop_after == ("C", l):
            break

        phw = ExitStack()
        wg = sb("w_g", [128, 8, DFF], BF16, phw)
        wu = sb("w_u", [128, 8, DFF], BF16, phw)
        wd = sb("w_d", [128, NFF, D], BF16, phw)
        with ExitStack() as ph:
            w_o = sb("w_o", [128, 8, D], BF16, ph)
            xt = [sb(f"d_xt{i}", [128, 8, TS], F32, ph) for i in range(2)]
            mx = [sb(f"d_mx{i}", [128, 8, TS], BF16, ph) for i in range(2)]
            dma("pool", w_o[:], wout_d[l].rearrange("(kc p) n -> p kc n", p=128), [], ["w_o"], "w_o")
            for wt, wsrc_ in ((wg, wg_d), (wu, wu_d)):
                wsrc = wsrc_[l].rearrange("(kc p) n -> p kc n", p=128)
                for part in range(2):
                    cs_ = slice(part * 1408, (part + 1) * 1408)
                    dma("pool", wt[:, :, cs_], wsrc[:, :, cs_], [], ["w_gu"], "w_gu")

            def d1_load(j):
                sl = j % 2
                dma("sp", xt[sl][:], xT_v[:, :, j * TS:(j + 1) * TS], [("xT", j)], [("d_xt", sl)], f"d_xt{sl}")
                dma("sp", mx[sl][:], mixT_v[:, :, j * TS:(j + 1) * TS], [], [("d_mx", sl)], f"d_mx{sl}")

            d1_load(0)
            for j in range(NT):
                sl = j % 2
                st = 1 if j == 8 else 0
                if j + 1 < NT:
                    d1_load(j + 1)
                for n_ in range(8):
                    b = n_ % 4
                    for kc in range(8):
                        mm(ps[b][:], w_o[:, kc, n_ * 128:(n_ + 1) * 128], mx[sl][:, kc, :], kc == 0, kc == 7,
                           ["w_o", ("d_mx", sl)], [pk(b)])
                    stt(xt[sl][:, n_, :], ps[b][:], mod_col(l, st, 2, n_), xt[sl][:, n_, :], ALU.mult, ALU.add,
                        [pk(b), ("d_xt", sl), "modsb"], [("d_xt", sl)])
                dma("sp", xT_v[:, :, j * TS:(j + 1) * TS], xt[sl][:], [("d_xt", sl)], [("xT", j)], f"d_xo{sl}")
        S.barrier()
        if stop_after == ("D1", l):
            phw.close()
            break

        with ExitStack() as ph:
            xt = [sb(f"e_xt{i}", [128, 8, TS], F32, ph) for i in range(2)]
            hT = sb("e_hT", [128, 8, TS], BF16, ph)
            lnm = sb("e_lnm", [128, TS], F32, ph)
            sgt = sb("e_sg", [128, 2, TS], F32, ph)
            actb = sb("e_act", [128, NFF, TS], BF16, ph)
            wsrc = wd_d[l].rearrange("(f p) n -> p f n", p=128)
            for part in range(2):
                dma("pool", wd[:, part * 11:(part + 1) * 11, :], wsrc[:, part * 11:(part + 1) * 11, :], [], ["w_d"], "w_d")
            def d2_load(j):
                sl = j % 2
                dma("sp", xt[sl][:], xT_v[:, :, j * TS:(j + 1) * TS], [("xT", j)], [("e_xt", sl)], f"e_xt{sl}")

            d2_load(0)
            for j in range(NT):
                sl = j % 2
                st = 1 if j == 8 else 0
                if j + 1 < NT:
                    d2_load(j + 1)
                norm_tile(xt[sl], ("e_xt", sl), hT, "e_hT", sgt, "e_sg", hT, "e_hT",
                          lambda c, st=st, l=l: gain_col(l, st, 1, c), lambda c, st=st, l=l: mod_col(l, st, 3, c), lnm)
                for f in range(NFF):
                    bg, bu = 2 + f % 2, 4 + f % 2
                    for kc in range(8):
                        mm(ps[bg][:], wg[:, kc, f * 128:(f + 1) * 128], hT[:, kc, :], kc == 0, kc == 7, ["w_gu", "e_hT"], [pk(bg)])
                    for kc in range(8):
                        mm(ps[bu][:], wu[:, kc, f * 128:(f + 1) * 128], hT[:, kc, :], kc == 0, kc == 7, ["w_gu", "e_hT"], [pk(bu)])
                    sk = ("e_sg", f % 2)
                    act(sgt[:, f % 2, :], ps[bg][:], AF.Silu, [pk(bg)], [sk])
                    tt("dve", actb[:, f, :], sgt[:, f % 2, :], ps[bu][:], ALU.mult, [sk, pk(bu)], ["e_act"])
                for n_ in range(8):
                    b = 6 + n_ % 2
                    for f in range(NFF):
                        mm(ps[b][:], wd[:, f, n_ * 128:(n_ + 1) * 128], actb[:, f, :], f == 0, f == NFF - 1,
                           ["w_d", "e_act"], [pk(b)])
                    stt(xt[sl][:, n_, :], ps[b][:], mod_col(l, st, 5, n_), xt[sl][:, n_, :], ALU.mult, ALU.add,
                        [pk(b), ("e_xt", sl), "modsb"], [("e_xt", sl)])
                dma("sp", xT_v[:, :, j * TS:(j + 1) * TS], xt[sl][:], [("e_xt", sl)], [("xT", j)], f"e_xo{sl}")
        S.barrier()
        phw.close()

    if stop_after is None:
        with ExitStack() as ph:
            xt = [sb(f"f_xt{i}", [128, 8, TS], F32, ph) for i in range(2)]
            sq = sb("f_sq", [128, 8, TS], BF16, ph)
            yT = sb("f_yT", [128, 8, TS], F32, ph)
            lnm = sb("f_lnm", [128, TS], F32, ph)
            yo = [sb(f"f_yo{i}", [128, 4, D], F32, ph) for i in range(2)]

            def f_load(j):
                sl = j % 2
                dma("sp", xt[sl][:], xT_v[:, :, j * TS:(j + 1) * TS], [("xT", j)], [("f_xt", sl)], f"f_xt{sl}")

            f_load(0)
            nf0 = CVL["normf"]
            for j in range(NT):
                sl = j % 2
                if j + 1 < NT:
                    f_load(j + 1)
                norm_tile(xt[sl], ("f_xt", sl), sq, "f_sq", None, None, yT, "f_yT",
                          lambda c: cv_sb[:, nf0 + c:nf0 + c + 1], None, lnm)
                k = 0
                for s in range(4):
                    for half in range(2):
                        b = 2 + k % 6
                        k += 1
                        for cc in range(4):
                            c = half * 4 + cc
                            tr(ps[b][:, cc * 128:(cc + 1) * 128], yT[:, c, s * 128:(s + 1) * 128], ["f_yT"], [pk(b)])
                        if half == 0:
                            act(yo[sl][:, s, 0:512], ps[b][:], AF.Copy, [pk(b)], [("f_yo", sl)])
                        else:
                            cp("dve", yo[sl][:, s, 512:1024], ps[b][:], [pk(b)], [("f_yo", sl)])
                if j < 8:
                    dst = ys_d[j * TS:(j + 1) * TS, :].rearrange("(s p) f -> p s f", p=128)
                else:
                    dst = yp_d.rearrange("(s p) f -> p s f", p=128)
                dma("sp", dst, yo[sl][:], [("f_yo", sl)], [], f"f_yo{sl}")
        S.barrier()

    nsem = S.emit(nc, es)
    es.close()
    return nc, nsem


def make_in_maps(inp, nlayers=L):
    import ml_dtypes
    f = lambda a: np.ascontiguousarray(np.asarray(a, dtype=np.float32))
    ident, CI, CF, E0, E7 = build_consts()
    E0 = E0.astype(ml_dtypes.bfloat16)
    E7 = E7.astype(ml_dtypes.bfloat16)
    rpadh = np.zeros((nlayers, NH, 15, 127), np.float32)
    rpadh[:, :, :, 48:79] = f(inp["rpb"][:nlayers])[:, :, ::-1, ::-1]
    rpadh = np.ascontiguousarray(rpadh.reshape(nlayers, NH * 15, 127))
    shared = {
        "w_mod": f(inp["w_mod"][:nlayers]), "w_in": f(inp["w_in"][:nlayers]), "w_out": f(inp["w_out"][:nlayers]),
        "w_gate": f(inp["w_gate"][:nlayers]), "w_up": f(inp["w_up"][:nlayers]), "w_down": f(inp["w_down"][:nlayers]),
        "rpadh": rpadh,
        "lru_wr": f(inp["lru_wr"][:nlayers]), "lru_wi": f(inp["lru_wi"][:nlayers]),
        "ident": ident, "CI": CI, "CF": CF, "E0": E0, "E7": E7,
    }
    xs, xp = f(inp["x_sample"]), f(inp["x_prompt"])
    ck, cvv = f(inp["cache_k"]), f(inp["cache_v"])
    st, cvec, cctx = f(inp["state_lru"]), f(inp["c"]), f(inp["c_ctx"])
    P = {k: f(inp[k]) for k in ("norm1", "norm2", "b_mod", "conv_w", "conv_b", "lru_a", "lru_br", "lru_bi", "norm_final")}
    maps = []
    for b in range(8):
        rows = np.zeros((NV, 128), np.float32)

        def put(name, vec):
            v = np.asarray(vec, np.float32).reshape(-1, 128)
            rows[CVL[name]:CVL[name] + v.shape[0]] = v

        for l in range(L):
            put(("norm1", l), P["norm1"][l])
            put(("norm2", l), P["norm2"][l])
            put(("bmod", l), P["b_mod"][l])
            for j in range(4):
                put(("convw", l, j), P["conv_w"][l, j])
            put(("convb", l), P["conv_b"][l])
            for d in range(2):
                put(("br", l, d), P["lru_br"][l, d])
                put(("bi", l, d), P["lru_bi"][l, d])
                put(("h0", l, d), st[b, l, d])
        put("lrua", P["lru_a"].reshape(-1))
        put("normf", P["norm_final"])
        put("c", cvec[b])
        put("cctx", cctx)
        m = dict(shared)
        m["cv"] = np.ascontiguousarray(rows.T)
        m["xs"] = xs[b]
        m["xp"] = xp[2 * b:2 * b + 2].reshape(2 * SEQ_P, D)
        m["ck"] = ck[b].reshape(L, 256, 512)
        m["cvv"] = cvv[b].reshape(L, 256, 512)
        maps.append(m)
    return maps


_NC_CACHE = {}


def kernel(**inputs):
    if "nc" not in _NC_CACHE:
        _NC_CACHE["nc"] = build_program()[0]
    nc = _NC_CACHE["nc"]
    maps = make_in_maps(inputs)
    res = run_bass_kernel_spmd(nc, maps, core_ids=list(range(8)))
    R = res.results
    y_prompt = np.stack([R[b // 2]["yp"].reshape(2, SEQ_P, D)[b % 2] for b in range(16)]).astype(np.float32)
    y_sample = np.stack([R[b]["ys"] for b in range(8)]).astype(np.float32)
    nk = np.concatenate([R[b]["nk"] for b in range(8)], 0).reshape(16, L, SEQ_P, NH, HD).astype(np.float32)
    nv = np.concatenate([R[b]["nv"] for b in range(8)], 0).reshape(16, L, SEQ_P, NH, HD).astype(np.float32)
    ns = np.concatenate([R[b]["ns"] for b in range(8)], 0).reshape(16, L, 2, 512).astype(np.float32)
    return (y_prompt, y_sample, nk, nv, ns)
```

```python
import numpy as np
from contextlib import ExitStack
import concourse.bass as bass
import concourse.mybir as mybir
from concourse.bass_utils import run_bass_kernel_spmd

F32 = mybir.dt.float32
BF16 = mybir.dt.bfloat16
ALU = mybir.AluOpType
AF = mybir.ActivationFunctionType

D = 1024
L = 4
NH = 8
HD = 64
DFF = 2816
NFF = DFF // 128
SEQ_S = 4096
SEQ_P = 256
TOK = SEQ_S + 2 * SEQ_P
TS = 512
NT = TOK // TS
INC = 2560
EPS = 1e-6
NEG = -30000.0
NW = 22


class Op:
    __slots__ = ("eng", "fn", "dma", "chan", "deps", "signals", "sigval")


class Sched:
    COMPUTE = ("pe", "act", "dve", "pool")

    def __init__(self):
        self.ops = []
        self.last_writer = {}
        self.readers = {}
        self.barrier_ops = []
        self.last_compute = {}
        self.last_dma = {}

    def add(self, eng, fn, reads=(), writes=(), dma=False, chan=None):
        op = Op()
        op.eng, op.fn, op.dma, op.chan = eng, fn, dma, chan
        op.signals = dma
        op.sigval = 0
        deps = {}
        for b in self.barrier_ops:
            deps[b] = deps.get(b, False)
        for k in reads:
            w = self.last_writer.get(k)
            if w is not None:
                deps[w] = True
            if isinstance(k, tuple) and k[0] == "ps":
                for r in self.readers.get(k, ()):
                    if r.eng != eng:
                        deps[r] = deps.get(r, False)
        for k in writes:
            w = self.last_writer.get(k)
            if w is not None:
                deps[w] = deps.get(w, False)
            for r in self.readers.get(k, ()):
                deps[r] = deps.get(r, False)
        keep = []
        for d, raw in deps.items():
            if d.dma or dma or d.eng != eng:
                keep.append(d)
            elif eng == "pool" or (raw and eng != "pe"):
                keep.append(d)
        op.deps = keep
        for d in keep:
            d.signals = True
        for k in reads:
            self.readers.setdefault(k, []).append(op)
        for k in writes:
            self.last_writer[k] = op
            self.readers[k] = []
        if dma:
            assert chan is not None
            self.last_dma[chan] = op
        else:
            self.last_compute[eng] = op
        self.ops.append(op)
        return op

    def barrier(self):
        self.barrier_ops = list(self.last_compute.values()) + list(self.last_dma.values())
        for b in self.barrier_ops:
            b.signals = True

    def emit(self, nc, es):
        chans = sorted({op.chan for op in self.ops if op.dma})
        sems = {}
        for e in self.COMPUTE:
            sems[("eng", e)] = es.enter_context(nc.semaphore("s_" + e))
        for c in chans:
            sems[("chan", c)] = es.enter_context(nc.semaphore("c_" + c))
        cnt = {}
        for op in self.ops:
            key = ("chan", op.chan) if op.dma else ("eng", op.eng)
            if op.dma:
                cnt[key] = cnt.get(key, 0) + 16
                op.sigval = cnt[key]
            elif op.signals:
                cnt[key] = cnt.get(key, 0) + 1
                op.sigval = cnt[key]
        final = dict(cnt)
        streams = {e: [] for e in ("pe", "act", "dve", "pool", "sp")}
        for op in self.ops:
            streams[op.eng].append(op)

        def run(name, e):
            waited = {}
            for op in streams[name]:
                need = {}
                for d in op.deps:
                    key = ("chan", d.chan) if d.dma else ("eng", d.eng)
                    if d.sigval > need.get(key, 0):
                        need[key] = d.sigval
                for key, v in need.items():
                    if waited.get(key, 0) < v:
                        e.wait_ge(sems[key], v)
                        waited[key] = v
                ins = op.fn(e)
                if op.dma:
                    ins.then_inc(sems[("chan", op.chan)], 16)
                elif op.signals:
                    ins.then_inc(sems[("eng", op.eng)], 1)
            if name == "sp":
                for key, v in final.items():
                    e.wait_ge(sems[key], v)

        with nc.Block() as block:
            @block.tensor
            def _(e):
                run("pe", e)

            @block.scalar
            def _(e):
                run("act", e)

            @block.vector
            def _(e):
                run("dve", e)

            @block.gpsimd
            def _(e):
                run("pool", e)

            @block.sync
            def _(e):
                run("sp", e)
        return len(chans) + 4


class ColVecs:
    def __init__(self):
        self.cols = {}
        self.n = 0

    def reg(self, name, k):
        self.cols[name] = self.n
        self.n += k

    def __getitem__(self, name):
        return self.cols[name]


def make_layout():
    cvl = ColVecs()
    for l in range(L):
        cvl.reg(("norm1", l), 8)
        cvl.reg(("norm2", l), 8)
        cvl.reg(("bmod", l), 48)
        for j in range(4):
            cvl.reg(("convw", l, j), 4)
        cvl.reg(("convb", l), 4)
    cvl.reg("lrua", L * 2 * 4)
    for l in range(L):
        for d in range(2):
            cvl.reg(("br", l, d), 4)
            cvl.reg(("bi", l, d), 4)
            cvl.reg(("h0", l, d), 4)
    cvl.reg("normf", 8)
    cvl.reg("c", 8)
    cvl.reg("cctx", 8)
    return cvl


CVL = make_layout()
NV = CVL.n


def build_consts():
    ident = np.eye(128, dtype=np.float32)
    kc = np.arange(64)[:, None]
    qc = np.arange(64)[None, :]
    cs = np.clip(qc - 8, 0, 48)
    colok = (kc >= cs) & (kc < cs + 16)
    CI = np.full((128, NW, 64), NEG, np.float32)
    CF = np.full((128, NW, 64), NEG, np.float32)
    for a in range(2):
        for w in range(NW):
            ri = 17 - w + a
            if 0 <= ri <= 14:
                CF[a * 64:(a + 1) * 64, w, :] = np.where(colok, 0.0, NEG)
            if 3 <= ri <= 10:
                CI[a * 64:(a + 1) * 64, w, :] = np.where(colok, 0.0, NEG)

    def edge(qr0, kr0):
        E = np.full((128, 6, 8, 64), NEG, np.float32)
        for t in range(6):
            for a in range(2):
                kr = kr0 + 2 * t + a
                for b in range(8):
                    qr = qr0 + b
                    rs = min(max(qr - 4, 0), 56)
                    if rs <= kr < rs + 8:
                        E[a * 64:(a + 1) * 64, t, b, :] = 0.0
        return E.reshape(128, 6, 512)

    return ident, CI, CF, edge(0, 0), edge(56, 52)


def build_program(nlayers=L, debug=False, stop_after=None):
    nc = bass.Bass("TRN2", target_bir_lowering=False)
    S = Sched()
    es = ExitStack()

    def din(name, shape, dt=F32):
        return nc.dram_tensor(name, list(shape), dt, kind="ExternalInput").ap()

    def dout(name, shape, dt=F32):
        return nc.dram_tensor(name, list(shape), dt, kind="ExternalOutput").ap()

    def dscr(name, shape, dt=F32):
        kind = "ExternalOutput" if debug else "Internal"
        return nc.dram_tensor(name, list(shape), dt, kind=kind).ap()

    xs_d = din("xs", [SEQ_S, D])
    xp_d = din("xp", [2 * SEQ_P, D])
    ck_d = din("ck", [L, 256, 512])
    cvv_d = din("cvv", [L, 256, 512])
    cv_d = din("cv", [128, NV])
    wmod_d = din("w_mod", [nlayers, D, 6 * D])
    win_d = din("w_in", [nlayers, D, INC])
    wout_d = din("w_out", [nlayers, D, D])
    wg_d = din("w_gate", [nlayers, D, DFF])
    wu_d = din("w_up", [nlayers, D, DFF])
    wd_d = din("w_down", [nlayers, DFF, D])
    rpb_d = din("rpadh", [nlayers, NH * 15, 127])
    wr_d = din("lru_wr", [nlayers, 2, 8, 64, 64])
    wi_d = din("lru_wi", [nlayers, 2, 8, 64, 64])
    ident_d = din("ident", [128, 128])
    CI_d = din("CI", [128, NW, 64])
    CF_d = din("CF", [128, NW, 64])
    E0_d = din("E0", [128, 6, 512], BF16)
    E7_d = din("E7", [128, 6, 512], BF16)

    ys_d = dout("ys", [SEQ_S, D])
    yp_d = dout("yp", [2 * SEQ_P, D])
    nk_d = dout("nk", [2, L, SEQ_P, 512])
    nv_d = dout("nv", [2, L, SEQ_P, 512])
    ns_d = dout("ns", [2, L, 2, 512])

    xT_d = dscr("xT", [8, 128, TOK])
    qT_d = dscr("qT", [4, 128, TOK], BF16)
    kT_d = dscr("kT", [4, 128, TOK], BF16)
    vt_d = dscr("vtok", [TOK, 1024], BF16)
    xbT_d = dscr("xbT", [4, 128, TOK])
    gyT_d = dscr("gyT", [4, 128, TOK])
    xcT_d = dscr("xcT", [4, 128, TOK])
    hfT_d = dscr("hfT", [4, 128, TOK])
    mixT_d = dscr("mixT", [8, 128, TOK], BF16)
    toe_d = dscr("toe", [NH * 15, 64, 64])

    xT_v = xT_d.rearrange("c p t -> p c t")
    qT_v = qT_d.rearrange("c p t -> p c t")
    kT_v = kT_d.rearrange("c p t -> p c t")
    xbT_v = xbT_d.rearrange("c p t -> p c t")
    gyT_v = gyT_d.rearrange("c p t -> p c t")
    xcT_v = xcT_d.rearrange("c p t -> p c t")
    hfT_v = hfT_d.rearrange("c p t -> p c t")
    mixT_v = mixT_d.rearrange("c p t -> p c t")

    uid = [0]

    def sb(name, shape, dt, stack):
        uid[0] += 1
        return stack.enter_context(nc.sbuf_tensor(f"sb{uid[0]}_{name}", list(shape), dt))

    ps = [es.enter_context(nc.psum_tensor(f"ps{i}", [128, 512], F32)) for i in range(8)]

    def pk(i):
        return ("ps", i)

    ident_sb = sb("ident_sb", [128, 128], F32, es)
    ones_bf = sb("ones_bf", [128, 128], BF16, es)
    cv_sb = sb("cv_sb", [128, NV], F32, es)
    modsb = sb("modsb", [128, L, 2, 48], F32, es)
    gains = sb("gains", [128, L, 2, 2, 8], F32, es)
    la_sb = sb("la_sb", [128, 2, L * 2 * 4], F32, es)
    sc_bf = sb("sc_bf", [128, 8, 2], BF16, es)

    def cvc(name, i=0):
        c0 = CVL[name] + i
        return cv_sb[:, c0:c0 + 1]

    def dma(eng, out, in_, reads, writes, chan, slow=False):
        if slow:
            S.add(eng, lambda e: e.dma_start(out=out, in_=in_, allow_slow_non_contiguous=True), reads=reads,
                  writes=writes, dma=True, chan=chan)
        else:
            S.add(eng, lambda e: e.dma_start(out=out, in_=in_), reads=reads, writes=writes, dma=True, chan=chan)

    def act(out, in_, func, reads, writes, bias=None, scale=None):
        kw = {}
        if bias is not None:
            kw["bias"] = bias
        if scale is not None:
            kw["scale"] = scale
        S.add("act", lambda e: e.activation(out=out, in_=in_, func=func, **kw), reads=reads, writes=writes)

    def mm(out, lhsT, rhs, start, stop, reads, writes):
        S.add("pe", lambda e: e.matmul(out, lhsT, rhs, start=start, stop=stop), reads=reads, writes=writes)


    def tt(eng, out, in0, in1, op, reads, writes):
        S.add(eng, lambda e: e.tensor_tensor(out, in0, in1, op), reads=reads, writes=writes)

    def stt(out, in0, scalar, in1, op0, op1, reads, writes):
        S.add("dve", lambda e: e.scalar_tensor_tensor(out, in0, scalar, in1, op0, op1), reads=reads, writes=writes)

    def tsc(eng, out, in0, s1, s2, op0, op1, reads, writes):
        if s2 is None:
            S.add(eng, lambda e: e.tensor_scalar(out, in0, s1, None, op0), reads=reads, writes=writes)
        else:
            S.add(eng, lambda e: e.tensor_scalar(out, in0, s1, s2, op0, op1), reads=reads, writes=writes)

    def cp(eng, out, in_, reads, writes):
        S.add(eng, lambda e: e.tensor_copy(out, in_), reads=reads, writes=writes)

    def tr(out, in_, reads, writes):
        S.add("pe", lambda e: e.transpose(out, in_, ident_sb[:]), reads=reads + ["ident"], writes=writes)

    def scan(out, d0, d1, init, reads, writes):
        S.add("dve", lambda e: e.tensor_tensor_scan(out, d0, d1, init, ALU.mult, ALU.add), reads=reads, writes=writes)

    def mset(ap, val, writes):
        S.add("pool", lambda e: e.memset(ap, val), writes=writes)

    dma("sp", ident_sb[:], ident_d, [], ["ident"], "ident")
    dma("sp", cv_sb[:], cv_d, [], ["cv"], "cv")
    mset(ones_bf[:], 1.0, ["ones"])

    with ExitStack() as ph:
        sc_f = sb("sc_f", [128, 2, 8], F32, ph)
        wm_bf = sb("wm_bf", [128, 8, 6 * D], BF16, ph)
        sp_t = sb("sp_t", [128, 32], F32, ph)
        act(sc_f[:, 0, :], cv_sb[:, CVL["c"]:CVL["c"] + 8], AF.Silu, ["cv"], ["sc_f"])
        act(sc_f[:, 1, :], cv_sb[:, CVL["cctx"]:CVL["cctx"] + 8], AF.Silu, ["cv"], ["sc_f"])
        for s in range(2):
            cp("dve", sc_bf[:, :, s], sc_f[:, s, :], ["sc_f"], ["sc_bf"])
        a0 = CVL["lrua"]
        act(sp_t[:], cv_sb[:, a0:a0 + 32], AF.Exp, ["cv"], ["sp_t"], scale=-1.0)
        act(sp_t[:], sp_t[:], AF.Ln, ["sp_t"], ["sp_t"], bias=1.0)
        tsc("dve", la_sb[:, 0, :], sp_t[:], -8.0, None, ALU.mult, None, ["sp_t"], ["la"])
        tsc("dve", la_sb[:, 1, :], sp_t[:], -16.0, None, ALU.mult, None, ["sp_t"], ["la"])
        def mod_finish(l, bank):
            psv = ps[bank][:, 0:96].rearrange("p (o s) -> p o s", s=2)
            b0 = CVL[("bmod", l)]
            for s_ in range(2):
                tt("dve", modsb[:, l, s_, :], psv[:, :, s_], cv_sb[:, b0:b0 + 48], ALU.add, [pk(bank), "cv"], ["modsb"])
            for s_ in range(2):
                for ni in range(2):
                    sc_off = 8 if ni == 0 else 32
                    n0 = CVL[("norm1", l)] if ni == 0 else CVL[("norm2", l)]
                    stt(gains[:, l, s_, ni, :], modsb[:, l, s_, sc_off:sc_off + 8], 1.0, cv_sb[:, n0:n0 + 8],
                        ALU.add, ALU.mult, ["modsb", "cv"], ["gains"])

        for l in range(1):
            wsrc = wmod_d[l].rearrange("(kc p) n -> p kc n", p=128)
            for part in range(4):
                cs_ = slice(part * 1536, (part + 1) * 1536)
                dma("pool", wm_bf[:, :, cs_], wsrc[:, :, cs_], [], ["wm"], "wm")
            for oc in range(48):
                for kc in range(8):
                    mm(ps[0][:, oc * 2:oc * 2 + 2], wm_bf[:, kc, oc * 128:(oc + 1) * 128], sc_bf[:, kc, :],
                       kc == 0, kc == 7, ["wm", "sc_bf"], [pk(0)])
            mod_finish(l, 0)
    S.barrier()

    def mod_col(l, s, which, c):
        return modsb[:, l, s, which * 8 + c:which * 8 + c + 1]

    def gain_col(l, s, ni, c):
        return gains[:, l, s, ni, c:c + 1]

    def norm_pre(xt, xkey, sq, sqkey):
        for hf in range(2):
            act(sq[:, hf * 4:(hf + 1) * 4, :], xt[:, hf * 4:(hf + 1) * 4, :], AF.Square, [xkey], [sqkey])

    def norm_mid(sq, sqkey, lnm):
        for c in range(8):
            mm(ps[0][:], ones_bf[:], sq[:, c, :], c == 0, c == 7, ["ones", sqkey], [pk(0)])
        act(lnm[:], ps[0][:], AF.Ln, [pk(0)], ["lnm"], bias=EPS, scale=1.0 / D)
        act(ps[1][:], lnm[:], AF.Exp, ["lnm"], [pk(1)], scale=-0.5)

    def norm_post(xt, xkey, tmp2, tmpkey, out, outkey, gain_fn, shift_fn):
        for c in range(8):
            if shift_fn is None:
                stt(out[:, c, :], xt[:, c, :], gain_fn(c), ps[1][:], ALU.mult, ALU.mult,
                    [xkey, pk(1), "gains", "cv"], [outkey])
            else:
                tk = (tmpkey, c % 2)
                stt(tmp2[:, c % 2, :], xt[:, c, :], gain_fn(c), ps[1][:], ALU.mult, ALU.mult,
                    [xkey, pk(1), "gains"], [tk])
                act(out[:, c, :], tmp2[:, c % 2, :], AF.Identity, [tk, "modsb"], [outkey], bias=shift_fn(c))

    def norm_tile(xt, xkey, sq, sqkey, tmp2, tmpkey, out, outkey, gain_fn, shift_fn, lnm):
        norm_pre(xt, xkey, sq, sqkey)
        norm_mid(sq, sqkey, lnm)
        norm_post(xt, xkey, tmp2, tmpkey, out, outkey, gain_fn, shift_fn)

    with ExitStack() as ph:
        xin = [sb(f"xin{i}", [128, 4, D], F32, ph) for i in range(2)]
        xst = [sb(f"xst{i}", [128, 8, TS], F32, ph) for i in range(2)]

        def p0_load(j):
            sl = j % 2
            if j < 8:
                src = xs_d[j * TS:(j + 1) * TS, :].rearrange("(s p) f -> p s f", p=128)
            else:
                src = xp_d.rearrange("(s p) f -> p s f", p=128)
            dma("sp", xin[sl][:], src, [], [("xin", sl)], f"xin{sl}")

        p0_load(0)
        for j in range(NT):
            sl = j % 2
            if j + 1 < NT:
                p0_load(j + 1)
            for c in range(8):
                for s in range(4):
                    tr(ps[c][:, s * 128:(s + 1) * 128], xin[sl][:, s, c * 128:(c + 1) * 128], [("xin", sl)], [pk(c)])
                if c % 2 == 0:
                    act(xst[sl][:, c, :], ps[c][:], AF.Copy, [pk(c)], [("xst", sl)])
                else:
                    cp("dve", xst[sl][:, c, :], ps[c][:], [pk(c)], [("xst", sl)])
            dma("sp", xT_v[:, :, j * TS:(j + 1) * TS], xst[sl][:], [("xst", sl)], [("xT", j)], f"xst{sl}")
    S.barrier()
    nl_run = 0 if stop_after == "p0" else nlayers

    for l in range(nl_run):
        with ExitStack() as ph:
            w_in = sb("w_in", [128, 8, INC], BF16, ph)
            xt = [sb(f"a_xt{i}", [128, 8, TS], F32, ph) for i in range(2)]
            tmp2 = sb("a_tmp", [128, 2, TS], F32, ph)
            hTs = [sb(f"a_hT{i}", [128, 8, TS], BF16, ph) for i in range(2)]
            lnm = sb("a_lnm", [128, TS], F32, ph)
            qk = [sb(f"a_qk{i}", [128, 8, TS], BF16, ph) for i in range(2)]
            vst = [sb(f"a_vst{i}", [128, 4, 1024], BF16, ph) for i in range(2)]
            for i in range(2):
                mset(vst[i][:], 1.0, [("a_vst", i)])
            xy = [sb(f"a_xy{i}", [128, 8, TS], F32, ph) for i in range(2)]
            kvf = sb("a_kvf", [128, 2, 4, 512], F32, ph)
            wsrc = win_d[l].rearrange("(kc p) n -> p kc n", p=128)
            for part in (0, 1, 3, 2):
                cs_ = slice(part * 640, (part + 1) * 640)
                dma("pool", w_in[:, :, cs_], wsrc[:, :, cs_], [], [("w_in", part)], f"w_in{part}")

            def wkeys(c0, c1):
                return [("w_in", p_) for p_ in range(c0 // 640, (c1 - 1) // 640 + 1)]

            def a_load(j):
                sl = j % 2
                dma("sp", xt[sl][:], xT_v[:, :, j * TS:(j + 1) * TS], [("xT", j)], [("a_xt", sl)], f"a_xt{sl}")

            a_load(0)
            a_load(1)
            bank = [0]

            def nxt():
                b = 2 + bank[0] % 5
                bank[0] += 1
                return b

            do_mod = (l + 1 < nlayers)
            if do_mod:
                wmc = [sb(f"a_wmc{i}", [128, 8, 512], BF16, ph) for i in range(2)]
                wmsrc = wmod_d[l + 1].rearrange("(kc p) n -> p kc n", p=128)
            mctr = [0]

            def mod_chunk_load(ck):
                slot = mctr[0] % 2
                mctr[0] += 1
                dma("pool", wmc[slot][:], wmsrc[:, :, ck * 512:(ck + 1) * 512], [], [("a_wmc", slot)], f"a_wmc{slot}")
                return slot

            def mod_chunk_mm(ck, slot):
                for ol in range(4):
                    oc = ck * 4 + ol
                    for kc in range(8):
                        mm(ps[7][:, oc * 2:oc * 2 + 2], wmc[slot][:, kc, ol * 128:(ol + 1) * 128], sc_bf[:, kc, :],
                           kc == 0, kc == 7, [("a_wmc", slot), "sc_bf"], [pk(7)])

            def a_norm(j, part):
                sl_ = j % 2
                st_ = 1 if j == 8 else 0
                hk = ("a_hT", sl_)
                if part == 0:
                    norm_pre(xt[sl_], ("a_xt", sl_), hTs[sl_], hk)
                else:
                    norm_mid(hTs[sl_], hk, lnm)
                    norm_post(xt[sl_], ("a_xt", sl_), tmp2, "a_tmp", hTs[sl_], hk,
                              lambda c, st_=st_, l=l: gain_col(l, st_, 0, c), lambda c, st_=st_, l=l: mod_col(l, st_, 0, c))

            a_norm(0, 0)
            a_norm(0, 1)
            for j in range(NT):
                sl = j % 2
                st = 1 if j == 8 else 0
                t0 = j * TS
                hT = hTs[sl]
                hkey = ("a_hT", sl)
                if j + 1 < NT:
                    a_norm(j + 1, 0)
                gi = 0
                pend = []
                if do_mod:
                    for ck in ([j] + ([9 + j] if j < 3 else [])):
                        pend.append((ck, mod_chunk_load(ck)))
                for n in list(range(0, 8)) + list(range(12, 20)):
                    if gi == 5 and j + 1 < NT:
                        a_norm(j + 1, 1)
                    if gi == 11:
                        for ck, slot in pend:
                            mod_chunk_mm(ck, slot)
                    gi += 1
                    b = nxt()
                    for kc in range(8):
                        mm(ps[b][:], w_in[:, kc, n * 128:(n + 1) * 128], hT[:, kc, :], kc == 0, kc == 7,
                           wkeys(n * 128, (n + 1) * 128) + [hkey], [pk(b)])
                    if n < 4:
                        act(qk[sl][:, n, :], ps[b][:], AF.Copy, [pk(b)], [("a_qk", sl)], scale=0.125)
                    elif n < 8:
                        cp("dve", qk[sl][:, n, :], ps[b][:], [pk(b)], [("a_qk", sl)])
                    elif n < 16:
                        cp("dve", xy[sl][:, n - 12, :], ps[b][:], [pk(b)], [("a_xy", sl)])
                    else:
                        act(xy[sl][:, n - 12, :], ps[b][:], AF.Gelu_apprx_tanh, [pk(b)], [("a_xy", sl)])
                import os as _os
                J8 = (j == 8) and not _os.environ.get('SKIP8')
                for which in ([1, 0] if J8 else [1]):
                    col0 = 1024 if which == 1 else 512
                    for s in range(4):
                        b = nxt()
                        for kc in range(8):
                            mm(ps[b][:], hT[:, kc, s * 128:(s + 1) * 128], w_in[:, kc, col0:col0 + 512],
                               kc == 0, kc == 7, wkeys(col0, col0 + 512) + [hkey], [pk(b)])
                        if J8:
                            if s % 2 == 1:
                                act(kvf[:, which, s, :], ps[b][:], AF.Copy, [pk(b)], ["a_kvf"])
                            else:
                                cp("dve", kvf[:, which, s, :], ps[b][:], [pk(b)], ["a_kvf"])
                            if which == 1:
                                kv5 = kvf[:, which, s, :].rearrange("p (h2 par d) -> p h2 par d", h2=4, par=2)
                                v5 = vst[sl][:, s, :].rearrange("p (h2 par x d) -> p h2 par x d", h2=4, par=2, x=2)
                                for par in range(2):
                                    cp("pool", v5[:, :, par, par, :], kv5[:, :, par, :], ["a_kvf", ("a_vst", sl)], [("a_vst", sl)])
                        elif which == 1:
                            p5 = ps[b][:].rearrange("p (h2 par d) -> p h2 par d", h2=4, par=2)
                            v5 = vst[sl][:, s, :].rearrange("p (h2 par x d) -> p h2 par x d", h2=4, par=2, x=2)
                            for par in range(2):
                                if (s + par) % 2 == 0:
                                    act(v5[:, :, par, par, :], p5[:, :, par, :], AF.Copy, [pk(b), ("a_vst", sl)], [("a_vst", sl)])
                                else:
                                    cp("dve", v5[:, :, par, par, :], p5[:, :, par, :], [pk(b), ("a_vst", sl)], [("a_vst", sl)])
                dma("sp", qT_v[:, :, t0:t0 + TS], qk[sl][:, 0:4, :], [("a_qk", sl)], [], f"a_qk{sl}")
                dma("sp", kT_v[:, :, t0:t0 + TS], qk[sl][:, 4:8, :], [("a_qk", sl)], [], f"a_qk{sl}")
                dma("sp", vt_d[t0:t0 + TS, :].rearrange("(s p) f -> p s f", p=128), vst[sl][:],
                    [("a_vst", sl)], [], f"a_vst{sl}")
                dma("sp", xbT_v[:, :, t0:t0 + TS], xy[sl][:, 0:4, :], [("a_xy", sl)], [], f"a_xy{sl}")
                dma("sp", gyT_v[:, :, t0:t0 + TS], xy[sl][:, 4:8, :], [("a_xy", sl)], [], f"a_xy{sl}")
                if j + 2 < NT:
                    a_load(j + 2)
                if J8:
                    for s_ in range(2):
                        dma("sp", nk_d[s_, l].rearrange("(h p) f -> p h f", p=128), kvf[:, 0, 2 * s_:2 * s_ + 2, :], ["a_kvf"], [], "a_kvf")
                        dma("sp", nv_d[s_, l].rearrange("(h p) f -> p h f", p=128), kvf[:, 1, 2 * s_:2 * s_ + 2, :], ["a_kvf"], [], "a_kvf")
            if do_mod:
                mod_finish(l + 1, 7)
        S.barrier()
        if stop_after == ("A", l):
            break

        with ExitStack() as ph:
            TI = sb("b_TI", [128, NH, NW, 64], F32, ph)
            TF = sb("b_TF", [128, NH, NW, 64], F32, ph)
            with ExitStack() as ph2:
                Tb = sb("b_Tb", [128, NH, NW, 64], F32, ph2)
                CIs = sb("b_CI", [128, NW, 64], F32, ph2)
                CFs = sb("b_CF", [128, NW, 64], F32, ph2)
                dma("sp", CIs[:], CI_d, [], ["b_CI"], "b_CI")
                dma("sp", CFs[:], CF_d, [], ["b_CF"], "b_CF")
                mset(Tb[:], 0.0, ["b_Tb"])
                src = bass.AP(rpb_d.tensor, l * (NH * 15 * 127) + 63, [[127, NH * 15], [-1, 64], [1, 64]])
                dma("sp", toe_d, src, [], ["toe"], "b_toe")
                for h in range(NH):
                    for a in range(2):
                        dma("sp", Tb[a * 64:(a + 1) * 64, h, 3 + a:18 + a, :],
                            toe_d[h * 15:(h + 1) * 15].rearrange("u k q -> k u q"), ["toe", "b_Tb"], ["b_Tb"], "b_Tb")
                for h in range(NH):
                    tt("dve", TI[:, h], Tb[:, h], CIs[:], ALU.add, ["b_Tb", "b_CI"], ["b_TI"])
                    tt("pool", TF[:, h], Tb[:, h], CFs[:], ALU.add, ["b_Tb", "b_CF"], ["b_TF"])
                S.barrier()
            E0s = sb("b_E0", [128, 6, 512], BF16, ph)
            E7s = sb("b_E7", [128, 6, 512], BF16, ph)
            dma("sp", E0s[:], E0_d, [], ["b_E0"], "b_E0")
            dma("sp", E7s[:], E7_d, [], ["b_E7"], "b_E7")
            kcT = sb("b_kcT", [128, 4, 256], BF16, ph)
            vcw = sb("b_vcw", [128, 2, 1024], BF16, ph)
            with ExitStack() as ph2:
                ckf = sb("b_ckf", [128, 2, 512], F32, ph2)
                cvf = sb("b_cvf", [128, 2, 512], F32, ph2)
                dma("sp", ckf[:], ck_d[l].rearrange("(t p) f -> p t f", p=128), [], ["b_ckf"], "b_ckf")
                dma("sp", cvf[:], cvv_d[l].rearrange("(t p) f -> p t f", p=128), [], ["b_cvf"], "b_cvf")
                mset(vcw[:], 1.0, ["b_vcw"])
                for t in range(2):
                    c5 = cvf[:, t, :].rearrange("p (h2 par d) -> p h2 par d", h2=4, par=2)
                    v5 = vcw[:, t, :].rearrange("p (h2 par x d) -> p h2 par x d", h2=4, par=2, x=2)
                    for par in range(2):
                        cp("dve", v5[:, :, par, par, :], c5[:, :, par, :], ["b_cvf", "b_vcw"], ["b_vcw"])
                for c in range(4):
                    for t in range(2):
                        tr(ps[c][:, t * 128:(t + 1) * 128], ckf[:, t, c * 128:(c + 1) * 128], ["b_ckf"], [pk(c)])
                    cp("dve", kcT[:, c, :], ps[c][:, 0:256], [pk(c)], ["b_kcT"])
                S.barrier()
            qt = [sb(f"b_qt{i}", [128, 4, TS], BF16, ph) for i in range(2)]
            kw = [sb(f"b_kw{i}", [128, 4, 1024], BF16, ph) for i in range(2)]
            vw = [sb(f"b_vw{i}", [128, 8, 1024], BF16, ph) for i in range(2)]
            ST = [sb(f"b_ST{i}", [128, TS], F32, ph) for i in range(3)]
            PT = [sb(f"b_PT{i}", [128, 10, TS], BF16, ph) for i in range(2)]
            lnd = [sb(f"b_lnd{i}", [128, TS], F32, ph) for i in range(1)] * 2
            rden = [sb(f"b_rden{i}", [128, TS], F32, ph) for i in range(1)] * 2
            att = [sb(f"b_att{i}", [128, 4, TS], BF16, ph) for i in range(2)]

            blocks = []
            for j in range(8):
                if j == 0:
                    r_lo, nk_, kind, dl = 0, 6, "E0", 0
                elif j == 7:
                    r_lo, nk_, kind, dl = 52, 6, "E7", -4
                else:
                    r_lo, nk_, kind, dl = 8 * j - 4, 8, "I", -4
                blocks.append(dict(q0=j * TS, n=TS, k0=r_lo * 64, nk=nk_, kind=kind, delta0=dl, ctx=True, tile=j, qoff=0))
            for s in range(2):
                blocks.append(dict(q0=SEQ_S + s * SEQ_P, n=SEQ_P, k0=SEQ_S + s * SEQ_P, nk=2, kind="N", delta0=0,
                                   ctx=False, tile=8, qoff=s * SEQ_P))

            def b_load(bi):
                bl = blocks[bi]
                sl = bi % 2
                n, nk_ = bl["n"], bl["nk"]
                dma("sp", qt[sl][:, :, 0:n], qT_v[:, :, bl["q0"]:bl["q0"] + n], [], [("b_qt", sl)], f"b_qt{sl}")
                dma("sp", kw[sl][:, :, 0:nk_ * 128], kT_v[:, :, bl["k0"]:bl["k0"] + nk_ * 128], [], [("b_kw", sl)],
                    f"b_kw{sl}")
                dma("sp", vw[sl][:, 0:nk_, :],
                    vt_d[bl["k0"]:bl["k0"] + nk_ * 128, :].rearrange("(t p) f -> p t f", p=128),
                    [], [("b_vw", sl)], f"b_vw{sl}")

            sctr = [0]

            def lhs_v(tensor_ap, t, h):
                return tensor_ap[:, t, h * 128:(h + 1) * 128]

            def qrange(bl, t):
                kind = bl["kind"]
                if kind == "N" or t >= bl["nk"]:
                    return 0, bl["n"]
                ok = []
                for b_ in range(8):
                    v = False
                    for a_ in range(2):
                        if kind == "I":
                            v = v or (3 <= 2 * t + 3 + a_ - b_ <= 10)
                        else:
                            qr = b_ if kind == "E0" else 56 + b_
                            kr = (2 * t + a_) if kind == "E0" else (52 + 2 * t + a_)
                            rs = min(max(qr - 4, 0), 56)
                            v = v or (rs <= kr < rs + 8)
                    if v:
                        ok.append(b_)
                return ok[0] * 64, (ok[-1] + 1) * 64

            def do_scores(bi, h, hs):
                bl = blocks[bi]
                sl = bi % 2
                n, nk_ = bl["n"], bl["nk"]
                c, po = h // 2, 64 * (h % 2)
                ntl = nk_ + (2 if bl["ctx"] else 0)
                for t in range(ntl):
                    b = 2 + sctr[0] % 4
                    sti = sctr[0] % 3
                    sctr[0] += 1
                    if t < nk_:
                        lhs = kw[sl][po:po + 64, c, t * 128:(t + 1) * 128]
                        rk = [("b_kw", sl)]
                    else:
                        tc_ = t - nk_
                        lhs = kcT[po:po + 64, c, tc_ * 128:(tc_ + 1) * 128]
                        rk = ["b_kcT"]
                    c0, c1 = qrange(bl, t)
                    mm(ps[b][:, c0:c1], lhs, qt[sl][po:po + 64, c, c0:c1], True, True, rk + [("b_qt", sl)], [pk(b)])
                    if t < nk_ and bl["kind"] != "N":
                        w0 = 10 - (bl["delta0"] + 2 * t)
                        tab = TI if bl["kind"] == "I" else TF
                        tv = tab[:, h, w0 + c0 // 64:w0 + c1 // 64, :].rearrange("p w q -> p (w q)")
                        tt("dve", ST[sti][:, c0:c1], ps[b][:, c0:c1], tv, ALU.add, [pk(b), "b_TI", "b_TF"], [("b_ST", sti)])
                        if bl["kind"] != "I":
                            Es = E0s if bl["kind"] == "E0" else E7s
                            tt("pool", ST[sti][:, c0:c1], ST[sti][:, c0:c1], Es[:, t, c0:c1], ALU.add,
                               [("b_ST", sti), "b_E0", "b_E7"], [("b_ST", sti)])
                        act(PT[hs][:, t, c0:c1], ST[sti][:, c0:c1], AF.Exp, [("b_ST", sti)], [("b_PT", hs)])
                    else:
                        act(PT[hs][:, t, c0:c1], ps[b][:, c0:c1], AF.Exp, [pk(b)], [("b_PT", hs)])

            def do_pv(bi, h, hs):
                bl = blocks[bi]
                sl = bi % 2
                asl = bl["tile"] % 2
                n, nk_ = bl["n"], bl["nk"]
                c, po = h // 2, 64 * (h % 2)
                pd = 64 - po
                ntl = nk_ + (2 if bl["ctx"] else 0)
                b = 6 + hs
                torder = list(range(nk_, ntl)) + list(range(nk_))
                for ti, t in enumerate(torder):
                    if t < nk_:
                        lhs = lhs_v(vw[sl], t, h)
                        rk = [("b_vw", sl)]
                    else:
                        lhs = lhs_v(vcw, t - nk_, h)
                        rk = ["b_vcw"]
                    c0, c1 = qrange(bl, t)
                    mm(ps[b][:, c0:c1], lhs, PT[hs][:, t, c0:c1], ti == 0, ti == ntl - 1, rk + [("b_PT", hs)], [pk(b)])
                act(lnd[hs][po:po + 64, 0:n], ps[b][pd:pd + 64, 0:n], AF.Ln, [pk(b)], ["b_lnd"])
                act(rden[hs][po:po + 64, 0:n], lnd[hs][po:po + 64, 0:n], AF.Exp, ["b_lnd"], ["b_rden"],
                    scale=-1.0)
                qo = bl["qoff"]
                tt("dve", att[asl][po:po + 64, c, qo:qo + n], ps[b][po:po + 64, 0:n], rden[hs][po:po + 64, 0:n],
                   ALU.mult, [pk(b), "b_rden"], [("b_att", asl)])

            b_load(0)
            b_load(1)
            units = [(bi, h) for bi in range(len(blocks)) for h in range(NH)]
            do_scores(units[0][0], units[0][1], 0)
            for ui, (bi, h) in enumerate(units):
                bl = blocks[bi]
                if ui + 1 < len(units):
                    nbi, nh = units[ui + 1]
                    do_scores(nbi, nh, (ui + 1) % 2)
                do_pv(bi, h, ui % 2)
                if h == NH - 1:
                    if bi + 2 < len(blocks):
                        b_load(bi + 2)
                    last_of_tile = (bi + 1 == len(blocks)) or (blocks[bi + 1]["tile"] != bl["tile"])
                    if last_of_tile:
                        tj = bl["tile"]
                        dma("sp", mixT_v[:, 0:4, tj * TS:(tj + 1) * TS], att[tj % 2][:], [("b_att", tj % 2)], [],
                            f"b_att{tj % 2}")
        S.barrier()
        if stop_after == ("B", l):
            break

        with ExitStack() as ph:
            wbd = sb("c_wbd", [128, 2, 2, 4, 128], BF16, ph)
            xbh = [sb(f"c_xbh{i}", [128, 4, TS + 4], F32, ph) for i in range(2)]
            xc = [sb(f"c_xc{i}", [128, 4, TS], F32, ph) for i in range(2)]
            gy = [sb(f"c_gy{i}", [128, 4, TS], F32, ph) for i in range(2)]
            xcb = [sb(f"c_xcb{i}", [128, 4, TS], BF16, ph) for i in range(2)]
            rr = [sb(f"c_r{i}", [128, 4, TS], F32, ph) for i in range(2)]
            ii = [sb(f"c_i{i}", [128, 4, TS], F32, ph) for i in range(2)]
            aa = [sb(f"c_a{i}", [128, 4, TS], F32, ph) for i in range(2)]
            ss = [sb(f"c_s{i}", [128, 4, TS], F32, ph) for i in range(2)]
            hfb = [sb(f"c_hf{i}", [128, 4, TS], F32, ph) for i in range(2)]
            hb = [sb(f"c_hb{i}", [128, 4, TS], F32, ph) for i in range(2)]
            rec = [sb(f"c_rec{i}", [128, 4, TS], BF16, ph) for i in range(2)]
            mset(wbd[:], 0.0, ["c_wbd"])
            for d in range(2):
                for g, wsrc_d in enumerate((wr_d, wi_d)):
                    for par in range(2):
                        src = wsrc_d[l, d, par::2].rearrange("n j k -> j n k")
                        dma("pool", wbd[par * 64:(par + 1) * 64, d, g, :, par * 64:(par + 1) * 64], src,
                            ["c_wbd"], ["c_wbd"], "c_wbd")
            segs = [dict(t0=j * TS, n=TS, left=(j > 0), right=(j < 7), sample=True, idx=j) for j in range(8)]
            segs += [dict(t0=SEQ_S + s * SEQ_P, n=SEQ_P, left=False, right=False, sample=False, idx=s) for s in range(2)]
            gctr = [0]

            def gates(d, xcs, xckey, n, p, imul=True):
                cp("pool", xcb[p][:, :, 0:n], xcs[:, :, 0:n], [xckey], [("c_xcb", p)])
                for c in range(4):
                    for g in range(2):
                        b = gctr[0] % 8
                        gctr[0] += 1
                        mm(ps[b][:, 0:n], wbd[:, d, g, c, :], xcb[p][:, c, 0:n], True, True, ["c_wbd", ("c_xcb", p)], [pk(b)])
                        dst = rr[p] if g == 0 else ii[p]
                        bname = ("br", l, d) if g == 0 else ("bi", l, d)
                        act(dst[:, c, 0:n], ps[b][:, 0:n], AF.Sigmoid, [pk(b), "cv"], [("c_r", p) if g == 0 else ("c_i", p)],
                            bias=cvc(bname, c))
                lcol = (l * 2 + d) * 4
                for c in range(4):
                    act(aa[p][:, c, 0:n], rr[p][:, c, 0:n], AF.Exp, [("c_r", p), "la"], [("c_a", p)],
                        scale=la_sb[:, 0, lcol + c:lcol + c + 1])
                    act(ss[p][:, c, 0:n], rr[p][:, c, 0:n], AF.Exp, [("c_r", p), "la"], [("c_s", p)],
                        scale=la_sb[:, 1, lcol + c:lcol + c + 1])
                for c in range(4):
                    act(ss[p][:, c, 0:n], ss[p][:, c, 0:n], AF.Sqrt, [("c_s", p)], [("c_s", p)], bias=1.0, scale=-1.0)
                if imul:
                    gates_imul(xcs, xckey, n, p)

            def gates_imul(xcs, xckey, n, p):
                for c in range(4):
                    tt("pool", ii[p][:, c, 0:n], ii[p][:, c, 0:n], xcs[:, c, 0:n], ALU.mult, [("c_i", p), xckey], [("c_i", p)])

            def gates_tail(xcs, xckey, n, p):
                for c in range(4):
                    tt("dve", ss[p][:, c, 0:n], ss[p][:, c, 0:n], ii[p][:, c, 0:n], ALU.mult, [("c_s", p), ("c_i", p)], [("c_s", p)])

            def cf_load(si):
                sg = segs[si]
                sl = si % 2
                n, t0 = sg["n"], sg["t0"]
                lo = 0 if sg["left"] else 2
                hi = n + 3 if sg["right"] else n + 2
                if not sg["left"]:
                    mset(xbh[sl][:, :, 0:2], 0.0, [("c_xbh", sl)])
                if not sg["right"]:
                    mset(xbh[sl][:, :, n + 2:n + 3], 0.0, [("c_xbh", sl)])
                dma("sp", xbh[sl][:, :, lo:hi], xbT_v[:, :, t0 - 2 + lo:t0 - 2 + hi], [("c_xbh", sl)], [("c_xbh", sl)],
                    f"c_xbh{sl}")

            def f_stageA(si):
                sg = segs[si]
                sl = si % 2
                n, t0 = sg["n"], sg["t0"]
                if si + 1 < len(segs):
                    cf_load(si + 1)
                xk = ("c_xc", sl)
                for c in range(4):
                    act(xc[sl][:, c, 0:n], xbh[sl][:, c, 0:n], AF.Identity, [("c_xbh", sl), "cv"], [xk],
                        bias=cvc(("convb", l), c), scale=cvc(("convw", l, 0), c))
                for jj in range(1, 4):
                    for c in range(4):
                        stt(xc[sl][:, c, 0:n], xbh[sl][:, c, jj:jj + n], cvc(("convw", l, jj), c), xc[sl][:, c, 0:n],
                            ALU.mult, ALU.add, [("c_xbh", sl), xk, "cv"], [xk])
                dma("sp", xcT_v[:, :, t0:t0 + n], xc[sl][:, :, 0:n], [xk], [], f"c_xc{sl}")
                gates(0, xc[sl], xk, n, sl)

            def f_stageB(si):
                sg = segs[si]
                sl = si % 2
                n, t0 = sg["n"], sg["t0"]
                gates_tail(xc[sl], ("c_xc", sl), n, sl)
                for c in range(4):
                    if sg["sample"] and sg["idx"] > 0:
                        init = hfb[1 - sl][:, c, TS - 1:TS]
                    elif sg["sample"]:
                        init = cvc(("h0", l, 0), c)
                    else:
                        init = 0.0
                    scan(hfb[sl][:, c, 0:n], aa[sl][:, c, 0:n], ss[sl][:, c, 0:n], init,
                         [("c_a", sl), ("c_s", sl), ("c_hf", 1 - sl), "cv"], [("c_hf", sl)])
                dma("sp", hfT_v[:, :, t0:t0 + n], hfb[sl][:, :, 0:n], [("c_hf", sl)], [], f"c_hfo{sl}")
                if not sg["sample"]:
                    dma("sp", ns_d[sg["idx"], l, 0].rearrange("(c p o) -> p c o", p=128, o=1),
                        hfb[sl][:, :, n - 1:n], [("c_hf", sl)], [], f"c_nsf{sl}", slow=True)

            cf_load(0)
            f_stageA(0)
            for si in range(len(segs)):
                if si + 1 < len(segs):
                    f_stageA(si + 1)
                f_stageB(si)
            S.barrier()
            order = list(range(7, -1, -1)) + [8, 9]

            def cb_load(oi):
                sg = segs[order[oi]]
                sl = oi % 2
                n, t0 = sg["n"], sg["t0"]
                dma("sp", xc[sl][:, :, 0:n], xcT_v[:, :, t0:t0 + n], [], [("c_xc", sl)], f"c_xc{sl}")
                dma("sp", gy[sl][:, :, 0:n], gyT_v[:, :, t0:t0 + n], [], [("c_gy", sl)], f"c_gy{sl}")
                dma("sp", hfb[sl][:, :, 0:n], hfT_v[:, :, t0:t0 + n], [], [("c_hf", sl)], f"c_hfi{sl}")

            def b_stageA(oi):
                sg = segs[order[oi]]
                sl = oi % 2
                gates(1, xc[sl], ("c_xc", sl), sg["n"], sl, imul=(oi == 0))

            def b_stageB(oi):
                sg = segs[order[oi]]
                sl = oi % 2
                n, t0 = sg["n"], sg["t0"]
                gates_tail(xc[sl], ("c_xc", sl), n, sl)
                for c in range(4):
                    if sg["sample"] and sg["idx"] < 7:
                        init = hb[1 - sl][:, c, 0:1]
                    elif sg["sample"]:
                        init = cvc(("h0", l, 1), c)
                    else:
                        init = 0.0
                    scan(hb[sl][:, c, 0:n][:, ::-1], aa[sl][:, c, 0:n][:, ::-1], ss[sl][:, c, 0:n][:, ::-1], init,
                         [("c_a", sl), ("c_s", sl), ("c_hb", 1 - sl), "cv"], [("c_hb", sl)])
                if not sg["sample"]:
                    dma("sp", ns_d[sg["idx"], l, 1].rearrange("(c p o) -> p c o", p=128, o=1),
                        hb[sl][:, :, 0:1], [("c_hb", sl)], [], f"c_nsb{sl}", slow=True)
                for c in range(4):
                    tt("pool", hfb[sl][:, c, 0:n], hfb[sl][:, c, 0:n], hb[sl][:, c, 0:n], ALU.add,
                       [("c_hf", sl), ("c_hb", sl)], [("c_hf", sl)])
                    tt("dve", rec[sl][:, c, 0:n], hfb[sl][:, c, 0:n], gy[sl][:, c, 0:n], ALU.mult,
                       [("c_hf", sl), ("c_gy", sl)], [("c_rec", sl)])
                dma("sp", mixT_v[:, 4:8, t0:t0 + n], rec[sl][:, :, 0:n], [("c_rec", sl)], [], f"c_rec{sl}")

            cb_load(0)
            cb_load(1)
            b_stageA(0)
            for oi in range(len(order)):
                if oi + 1 < len(order):
                    b_stageA(oi + 1)
                b_stageB(oi)
                if oi + 1 < len(order):
                    sg1 = segs[order[oi + 1]]
                    gates_imul(xc[(oi + 1) % 2], ("c_xc", (oi + 1) % 2), sg1["n"], (oi + 1) % 2)
                if oi + 2 < len(order):
                    cb_load(oi + 2)
        S.barrier()
        if stop_after == ("C", l):
            break

        phw = ExitStack()
        wg = sb("w_g", [128, 8, DFF], BF16, phw)
        wu = sb("w_u", [128, 8, DFF], BF16, phw)
        wd = sb("w_d", [128, NFF, D], BF16, phw)
        with ExitStack() as ph:
            w_o = sb("w_o", [128, 8, D], BF16, ph)
            xt = [sb(f"d_xt{i}", [128, 8, TS], F32, ph) for i in range(2)]
            mx = [sb(f"d_mx{i}", [128, 8, TS], BF16, ph) for i in range(2)]
            dma("pool", w_o[:], wout_d[l].rearrange("(kc p) n -> p kc n", p=128), [], ["w_o"], "w_o")
            for wt, wsrc_ in ((wg, wg_d), (wu, wu_d)):
                wsrc = wsrc_[l].rearrange("(kc p) n -> p kc n", p=128)
                for part in range(2):
                    cs_ = slice(part * 1408, (part + 1) * 1408)
                    dma("pool", wt[:, :, cs_], wsrc[:, :, cs_], [], ["w_gu"], "w_gu")

            def d1_load(j):
                sl = j % 2
                dma("sp", xt[sl][:], xT_v[:, :, j * TS:(j + 1) * TS], [("xT", j)], [("d_xt", sl)], f"d_xt{sl}")
                dma("sp", mx[sl][:], mixT_v[:, :, j * TS:(j + 1) * TS], [], [("d_mx", sl)], f"d_mx{sl}")

            d1_load(0)
            for j in range(NT):
                sl = j % 2
                st = 1 if j == 8 else 0
                if j + 1 < NT:
                    d1_load(j + 1)
                for n_ in range(8):
                    b = n_ % 4
                    for kc in range(8):
                        mm(ps[b][:], w_o[:, kc, n_ * 128:(n_ + 1) * 128], mx[sl][:, kc, :], kc == 0, kc == 7,
                           ["w_o", ("d_mx", sl)], [pk(b)])
                    stt(xt[sl][:, n_, :], ps[b][:], mod_col(l, st, 2, n_), xt[sl][:, n_, :], ALU.mult, ALU.add,
                        [pk(b), ("d_xt", sl), "modsb"], [("d_xt", sl)])
                dma("sp", xT_v[:, :, j * TS:(j + 1) * TS], xt[sl][:], [("d_xt", sl)], [("xT", j)], f"d_xo{sl}")
        S.barrier()
        if stop_after == ("D1", l):
            phw.close()
            break

        with ExitStack() as ph:
            xt = [sb(f"e_xt{i}", [128, 8, TS], F32, ph) for i in range(2)]
            hT = sb("e_hT", [128, 8, TS], BF16, ph)
            lnm = sb("e_lnm", [128, TS], F32, ph)
            sgt = sb("e_sg", [128, 2, TS], F32, ph)
            actb = sb("e_act", [128, NFF, TS], BF16, ph)
            wsrc = wd_d[l].rearrange("(f p) n -> p f n", p=128)
            for part in range(2):
                dma("pool", wd[:, part * 11:(part + 1) * 11, :], wsrc[:, part * 11:(part + 1) * 11, :], [], ["w_d"], "w_d")
            def d2_load(j):
                sl = j % 2
                dma("sp", xt[sl][:], xT_v[:, :, j * TS:(j + 1) * TS], [("xT", j)], [("e_xt", sl)], f"e_xt{sl}")

            d2_load(0)

            def e_norm(j, part):
                sl_ = j % 2
                st_ = 1 if j == 8 else 0
                if part == 0:
                    norm_pre(xt[sl_], ("e_xt", sl_), hT, "e_hT")
                else:
                    norm_mid(hT, "e_hT", lnm)
                    norm_post(xt[sl_], ("e_xt", sl_), sgt, "e_sg", hT, "e_hT",
                              lambda c, st_=st_, l=l: gain_col(l, st_, 1, c), lambda c, st_=st_, l=l: mod_col(l, st_, 3, c))

            if NT > 1:
                d2_load(1)
            e_norm(0, 0)
            e_norm(0, 1)
            for j in range(NT):
                sl = j % 2
                st = 1 if j == 8 else 0
                for f in range(NFF):
                    bg, bu = 2 + f % 2, 4 + f % 2
                    for kc in range(8):
                        mm(ps[bg][:], wg[:, kc, f * 128:(f + 1) * 128], hT[:, kc, :], kc == 0, kc == 7, ["w_gu", "e_hT"], [pk(bg)])
                    for kc in range(8):
                        mm(ps[bu][:], wu[:, kc, f * 128:(f + 1) * 128], hT[:, kc, :], kc == 0, kc == 7, ["w_gu", "e_hT"], [pk(bu)])
                    sk = ("e_sg", f % 2)
                    act(sgt[:, f % 2, :], ps[bg][:], AF.Silu, [pk(bg)], [sk])
                    tt("dve", actb[:, f, :], sgt[:, f % 2, :], ps[bu][:], ALU.mult, [sk, pk(bu)], ["e_act"])
                if j + 1 < NT:
                    e_norm(j + 1, 0)
                for n_ in range(8):
                    if n_ == 2 and j + 1 < NT:
                        e_norm(j + 1, 1)
                    b = 6 + n_ % 2
                    for f in range(NFF):
                        mm(ps[b][:], wd[:, f, n_ * 128:(n_ + 1) * 128], actb[:, f, :], f == 0, f == NFF - 1,
                           ["w_d", "e_act"], [pk(b)])
                    stt(xt[sl][:, n_, :], ps[b][:], mod_col(l, st, 5, n_), xt[sl][:, n_, :], ALU.mult, ALU.add,
                        [pk(b), ("e_xt", sl), "modsb"], [("e_xt", sl)])
                dma("sp", xT_v[:, :, j * TS:(j + 1) * TS], xt[sl][:], [("e_xt", sl)], [("xT", j)], f"e_xo{sl}")
                if j + 2 < NT:
                    d2_load(j + 2)
        S.barrier()
        phw.close()

    if stop_after is None:
        with ExitStack() as ph:
            xt = [sb(f"f_xt{i}", [128, 8, TS], F32, ph) for i in range(2)]
            sq = sb("f_sq", [128, 8, TS], BF16, ph)
            yT = sb("f_yT", [128, 8, TS], F32, ph)
            lnm = sb("f_lnm", [128, TS], F32, ph)
            yo = [sb(f"f_yo{i}", [128, 4, D], F32, ph) for i in range(2)]

            def f_load(j):
                sl = j % 2
                dma("sp", xt[sl][:], xT_v[:, :, j * TS:(j + 1) * TS], [("xT", j)], [("f_xt", sl)], f"f_xt{sl}")

            f_load(0)
            nf0 = CVL["normf"]
            for j in range(NT):
                sl = j % 2
                if j + 1 < NT:
                    f_load(j + 1)
                norm_tile(xt[sl], ("f_xt", sl), sq, "f_sq", None, None, yT, "f_yT",
                          lambda c: cv_sb[:, nf0 + c:nf0 + c + 1], None, lnm)
                k = 0
                for s in range(4):
                    for half in range(2):
                        b = 2 + k % 6
                        k += 1
                        for cc in range(4):
                            c = half * 4 + cc
                            tr(ps[b][:, cc * 128:(cc + 1) * 128], yT[:, c, s * 128:(s + 1) * 128], ["f_yT"], [pk(b)])
                        if half == 0:
                            act(yo[sl][:, s, 0:512], ps[b][:], AF.Copy, [pk(b)], [("f_yo", sl)])
                        else:
                            cp("dve", yo[sl][:, s, 512:1024], ps[b][:], [pk(b)], [("f_yo", sl)])
                if j < 8:
                    dst = ys_d[j * TS:(j + 1) * TS, :].rearrange("(s p) f -> p s f", p=128)
                else:
                    dst = yp_d.rearrange("(s p) f -> p s f", p=128)
                dma("sp", dst, yo[sl][:], [("f_yo", sl)], [], f"f_yo{sl}")
        S.barrier()

    nsem = S.emit(nc, es)
    es.close()
    return nc, nsem


def make_in_maps(inp, nlayers=L):
    import ml_dtypes
    f = lambda a: np.ascontiguousarray(np.asarray(a, dtype=np.float32))
    ident, CI, CF, E0, E7 = build_consts()
    E0 = E0.astype(ml_dtypes.bfloat16)
    E7 = E7.astype(ml_dtypes.bfloat16)
    rpadh = np.zeros((nlayers, NH, 15, 127), np.float32)
    rpadh[:, :, :, 48:79] = f(inp["rpb"][:nlayers])[:, :, ::-1, ::-1]
    rpadh = np.ascontiguousarray(rpadh.reshape(nlayers, NH * 15, 127))
    shared = {
        "w_mod": f(inp["w_mod"][:nlayers]), "w_in": f(inp["w_in"][:nlayers]), "w_out": f(inp["w_out"][:nlayers]),
        "w_gate": f(inp["w_gate"][:nlayers]), "w_up": f(inp["w_up"][:nlayers]), "w_down": f(inp["w_down"][:nlayers]),
        "rpadh": rpadh,
        "lru_wr": f(inp["lru_wr"][:nlayers]), "lru_wi": f(inp["lru_wi"][:nlayers]),
        "ident": ident, "CI": CI, "CF": CF, "E0": E0, "E7": E7,
    }
    xs, xp = f(inp["x_sample"]), f(inp["x_prompt"])
    ck, cvv = f(inp["cache_k"]), f(inp["cache_v"])
    st, cvec, cctx = f(inp["state_lru"]), f(inp["c"]), f(inp["c_ctx"])
    P = {k: f(inp[k]) for k in ("norm1", "norm2", "b_mod", "conv_w", "conv_b", "lru_a", "lru_br", "lru_bi", "norm_final")}
    maps = []
    for b in range(8):
        rows = np.zeros((NV, 128), np.float32)

        def put(name, vec):
            v = np.asarray(vec, np.float32).reshape(-1, 128)
            rows[CVL[name]:CVL[name] + v.shape[0]] = v

        for l in range(L):
            put(("norm1", l), P["norm1"][l])
            put(("norm2", l), P["norm2"][l])
            put(("bmod", l), P["b_mod"][l])
            for j in range(4):
                put(("convw", l, j), P["conv_w"][l, j])
            put(("convb", l), P["conv_b"][l])
            for d in range(2):
                put(("br", l, d), P["lru_br"][l, d])
                put(("bi", l, d), P["lru_bi"][l, d])
                put(("h0", l, d), st[b, l, d])
        put("lrua", P["lru_a"].reshape(-1))
        put("normf", P["norm_final"])
        put("c", cvec[b])
        put("cctx", cctx)
        m = dict(shared)
        m["cv"] = np.ascontiguousarray(rows.T)
        m["xs"] = xs[b]
        m["xp"] = xp[2 * b:2 * b + 2].reshape(2 * SEQ_P, D)
        m["ck"] = ck[b].reshape(L, 256, 512)
        m["cvv"] = cvv[b].reshape(L, 256, 512)
        maps.append(m)
    return maps


_NC_CACHE = {}


def kernel(**inputs):
    if "nc" not in _NC_CACHE:
        _NC_CACHE["nc"] = build_program()[0]
    nc = _NC_CACHE["nc"]
    maps = make_in_maps(inputs)
    res = run_bass_kernel_spmd(nc, maps, core_ids=list(range(8)))
    R = res.results
    y_prompt = np.stack([R[b // 2]["yp"].reshape(2, SEQ_P, D)[b % 2] for b in range(16)]).astype(np.float32)
    y_sample = np.stack([R[b]["ys"] for b in range(8)]).astype(np.float32)
    nk = np.concatenate([R[b]["nk"] for b in range(8)], 0).reshape(16, L, SEQ_P, NH, HD).astype(np.float32)
    nv = np.concatenate([R[b]["nv"] for b in range(8)], 0).reshape(16, L, SEQ_P, NH, HD).astype(np.float32)
    ns = np.concatenate([R[b]["ns"] for b in range(8)], 0).reshape(16, L, 2, 512).astype(np.float32)
    return (y_prompt, y_sample, nk, nv, ns)
```

```python
import numpy as np
from contextlib import ExitStack
import concourse.bass as bass
import concourse.mybir as mybir
from concourse.bass_utils import run_bass_kernel_spmd

F32 = mybir.dt.float32
BF16 = mybir.dt.bfloat16
ALU = mybir.AluOpType
AF = mybir.ActivationFunctionType

D = 1024
L = 4
NH = 8
HD = 64
DFF = 2816
NFF = DFF // 128
SEQ_S = 4096
SEQ_P = 256
TOK = SEQ_S + 2 * SEQ_P
TS = 512
NT = TOK // TS
INC = 2560
EPS = 1e-6
NEG = -30000.0
NW = 22


class Op:
    __slots__ = ("eng", "fn", "dma", "chan", "deps", "signals", "sigval")


class Sched:
    COMPUTE = ("pe", "act", "dve", "pool")

    def __init__(self):
        self.ops = []
        self.last_writer = {}
        self.readers = {}
        self.barrier_ops = []
        self.last_compute = {}
        self.last_dma = {}

    def add(self, eng, fn, reads=(), writes=(), dma=False, chan=None):
        op = Op()
        op.eng, op.fn, op.dma, op.chan = eng, fn, dma, chan
        op.signals = dma
        op.sigval = 0
        deps = {}
        for b in self.barrier_ops:
            deps[b] = deps.get(b, False)
        for k in reads:
            w = self.last_writer.get(k)
            if w is not None:
                deps[w] = True
            if isinstance(k, tuple) and k[0] == "ps":
                for r in self.readers.get(k, ()):
                    if r.eng != eng:
                        deps[r] = deps.get(r, False)
        for k in writes:
            w = self.last_writer.get(k)
            if w is not None:
                deps[w] = deps.get(w, False)
            for r in self.readers.get(k, ()):
                deps[r] = deps.get(r, False)
        keep = []
        for d, raw in deps.items():
            if d.dma or dma or d.eng != eng:
                keep.append(d)
            elif eng == "pool" or (raw and eng != "pe"):
                keep.append(d)
        op.deps = keep
        for d in keep:
            d.signals = True
        for k in reads:
            self.readers.setdefault(k, []).append(op)
        for k in writes:
            self.last_writer[k] = op
            self.readers[k] = []
        if dma:
            assert chan is not None
            self.last_dma[chan] = op
        else:
            self.last_compute[eng] = op
        self.ops.append(op)
        return op

    def barrier(self):
        self.barrier_ops = list(self.last_compute.values()) + list(self.last_dma.values())
        for b in self.barrier_ops:
            b.signals = True

    def emit(self, nc, es):
        chans = sorted({op.chan for op in self.ops if op.dma})
        sems = {}
        for e in self.COMPUTE:
            sems[("eng", e)] = es.enter_context(nc.semaphore("s_" + e))
        for c in chans:
            sems[("chan", c)] = es.enter_context(nc.semaphore("c_" + c))
        cnt = {}
        for op in self.ops:
            key = ("chan", op.chan) if op.dma else ("eng", op.eng)
            if op.dma:
                cnt[key] = cnt.get(key, 0) + 16
                op.sigval = cnt[key]
            elif op.signals:
                cnt[key] = cnt.get(key, 0) + 1
                op.sigval = cnt[key]
        final = dict(cnt)
        streams = {e: [] for e in ("pe", "act", "dve", "pool", "sp")}
        for op in self.ops:
            streams[op.eng].append(op)

        def run(name, e):
            waited = {}
            for op in streams[name]:
                need = {}
                for d in op.deps:
                    key = ("chan", d.chan) if d.dma else ("eng", d.eng)
                    if d.sigval > need.get(key, 0):
                        need[key] = d.sigval
                for key, v in need.items():
                    if waited.get(key, 0) < v:
                        e.wait_ge(sems[key], v)
                        waited[key] = v
                ins = op.fn(e)
                if op.dma:
                    ins.then_inc(sems[("chan", op.chan)], 16)
                elif op.signals:
                    ins.then_inc(sems[("eng", op.eng)], 1)
            if name == "sp":
                for key, v in final.items():
                    e.wait_ge(sems[key], v)

        with nc.Block() as block:
            @block.tensor
            def _(e):
                run("pe", e)

            @block.scalar
            def _(e):
                run("act", e)

            @block.vector
            def _(e):
                run("dve", e)

            @block.gpsimd
            def _(e):
                run("pool", e)

            @block.sync
            def _(e):
                run("sp", e)
        return len(chans) + 4


class ColVecs:
    def __init__(self):
        self.cols = {}
        self.n = 0

    def reg(self, name, k):
        self.cols[name] = self.n
        self.n += k

    def __getitem__(self, name):
        return self.cols[name]


def make_layout():
    cvl = ColVecs()
    for l in range(L):
        cvl.reg(("norm1", l), 8)
        cvl.reg(("norm2", l), 8)
        cvl.reg(("bmod", l), 48)
        for j in range(4):
            cvl.reg(("convw", l, j), 4)
        cvl.reg(("convb", l), 4)
    cvl.reg("lrua", L * 2 * 4)
    for l in range(L):
        for d in range(2):
            cvl.reg(("br", l, d), 4)
            cvl.reg(("bi", l, d), 4)
            cvl.reg(("h0", l, d), 4)
    cvl.reg("normf", 8)
    cvl.reg("c", 8)
    cvl.reg("cctx", 8)
    return cvl


CVL = make_layout()
NV = CVL.n


def build_consts():
    ident = np.eye(128, dtype=np.float32)
    kc = np.arange(64)[:, None]
    qc = np.arange(64)[None, :]
    cs = np.clip(qc - 8, 0, 48)
    colok = (kc >= cs) & (kc < cs + 16)
    CI = np.full((128, NW, 64), NEG, np.float32)
    CF = np.full((128, NW, 64), NEG, np.float32)
    for a in range(2):
        for w in range(NW):
            ri = 17 - w + a
            if 0 <= ri <= 14:
                CF[a * 64:(a + 1) * 64, w, :] = np.where(colok, 0.0, NEG)
            if 3 <= ri <= 10:
                CI[a * 64:(a + 1) * 64, w, :] = np.where(colok, 0.0, NEG)

    def edge(qr0, kr0):
        E = np.full((128, 6, 8, 64), NEG, np.float32)
        for t in range(6):
            for a in range(2):
                kr = kr0 + 2 * t + a
                for b in range(8):
                    qr = qr0 + b
                    rs = min(max(qr - 4, 0), 56)
                    if rs <= kr < rs + 8:
                        E[a * 64:(a + 1) * 64, t, b, :] = 0.0
        return E.reshape(128, 6, 512)

    return ident, CI, CF, edge(0, 0), edge(56, 52)


def build_program(nlayers=L, debug=False, stop_after=None):
    nc = bass.Bass("TRN2", target_bir_lowering=False)
    S = Sched()
    es = ExitStack()

    def din(name, shape, dt=F32):
        return nc.dram_tensor(name, list(shape), dt, kind="ExternalInput").ap()

    def dout(name, shape, dt=F32):
        return nc.dram_tensor(name, list(shape), dt, kind="ExternalOutput").ap()

    def dscr(name, shape, dt=F32):
        kind = "ExternalOutput" if debug else "Internal"
        return nc.dram_tensor(name, list(shape), dt, kind=kind).ap()

    xs_d = din("xs", [SEQ_S, D])
    xp_d = din("xp", [2 * SEQ_P, D])
    ck_d = din("ck", [L, 256, 512])
    cvv_d = din("cvv", [L, 256, 512])
    cv_d = din("cv", [128, NV])
    wmod_d = din("w_mod", [nlayers, D, 6 * D])
    win_d = din("w_in", [nlayers, D, INC])
    wout_d = din("w_out", [nlayers, D, D])
    wg_d = din("w_gate", [nlayers, D, DFF])
    wu_d = din("w_up", [nlayers, D, DFF])
    wd_d = din("w_down", [nlayers, DFF, D])
    rpb_d = din("rpadh", [nlayers, NH * 15, 127])
    wr_d = din("lru_wr", [nlayers, 2, 8, 64, 64])
    wi_d = din("lru_wi", [nlayers, 2, 8, 64, 64])
    ident_d = din("ident", [128, 128])
    CI_d = din("CI", [128, NW, 64])
    CF_d = din("CF", [128, NW, 64])
    E0_d = din("E0", [128, 6, 512], BF16)
    E7_d = din("E7", [128, 6, 512], BF16)

    ys_d = dout("ys", [SEQ_S, D])
    yp_d = dout("yp", [2 * SEQ_P, D])
    nk_d = dout("nk", [2, L, SEQ_P, 512])
    nv_d = dout("nv", [2, L, SEQ_P, 512])
    ns_d = dout("ns", [2, L, 2, 512])

    xT_d = dscr("xT", [8, 128, TOK])
    qT_d = dscr("qT", [4, 128, TOK], BF16)
    kT_d = dscr("kT", [4, 128, TOK], BF16)
    vt_d = dscr("vtok", [TOK, 1024], BF16)
    xbT_d = dscr("xbT", [4, 128, TOK])
    gyT_d = dscr("gyT", [4, 128, TOK])
    xcT_d = dscr("xcT", [4, 128, TOK])
    hfT_d = dscr("hfT", [4, 128, TOK])
    mixT_d = dscr("mixT", [8, 128, TOK], BF16)
    toe_d = dscr("toe", [NH * 15, 64, 64])

    xT_v = xT_d.rearrange("c p t -> p c t")
    qT_v = qT_d.rearrange("c p t -> p c t")
    kT_v = kT_d.rearrange("c p t -> p c t")
    xbT_v = xbT_d.rearrange("c p t -> p c t")
    gyT_v = gyT_d.rearrange("c p t -> p c t")
    xcT_v = xcT_d.rearrange("c p t -> p c t")
    hfT_v = hfT_d.rearrange("c p t -> p c t")
    mixT_v = mixT_d.rearrange("c p t -> p c t")

    uid = [0]

    def sb(name, shape, dt, stack):
        uid[0] += 1
        return stack.enter_context(nc.sbuf_tensor(f"sb{uid[0]}_{name}", list(shape), dt))

    ps = [es.enter_context(nc.psum_tensor(f"ps{i}", [128, 512], F32)) for i in range(8)]

    def pk(i):
        return ("ps", i)

    ident_sb = sb("ident_sb", [128, 128], F32, es)
    ones_bf = sb("ones_bf", [128, 128], BF16, es)
    cv_sb = sb("cv_sb", [128, NV], F32, es)
    modsb = sb("modsb", [128, L, 2, 48], F32, es)
    gains = sb("gains", [128, L, 2, 2, 8], F32, es)
    la_sb = sb("la_sb", [128, 2, L * 2 * 4], F32, es)
    sc_bf = sb("sc_bf", [128, 8, 2], BF16, es)

    def cvc(name, i=0):
        c0 = CVL[name] + i
        return cv_sb[:, c0:c0 + 1]

    def dma(eng, out, in_, reads, writes, chan, slow=False):
        if slow:
            S.add(eng, lambda e: e.dma_start(out=out, in_=in_, allow_slow_non_contiguous=True), reads=reads,
                  writes=writes, dma=True, chan=chan)
        else:
            S.add(eng, lambda e: e.dma_start(out=out, in_=in_), reads=reads, writes=writes, dma=True, chan=chan)

    def act(out, in_, func, reads, writes, bias=None, scale=None):
        kw = {}
        if bias is not None:
            kw["bias"] = bias
        if scale is not None:
            kw["scale"] = scale
        S.add("act", lambda e: e.activation(out=out, in_=in_, func=func, **kw), reads=reads, writes=writes)

    def mm(out, lhsT, rhs, start, stop, reads, writes):
        S.add("pe", lambda e: e.matmul(out, lhsT, rhs, start=start, stop=stop), reads=reads, writes=writes)


    def tt(eng, out, in0, in1, op, reads, writes):
        S.add(eng, lambda e: e.tensor_tensor(out, in0, in1, op), reads=reads, writes=writes)

    def stt(out, in0, scalar, in1, op0, op1, reads, writes):
        S.add("dve", lambda e: e.scalar_tensor_tensor(out, in0, scalar, in1, op0, op1), reads=reads, writes=writes)

    def tsc(eng, out, in0, s1, s2, op0, op1, reads, writes):
        if s2 is None:
            S.add(eng, lambda e: e.tensor_scalar(out, in0, s1, None, op0), reads=reads, writes=writes)
        else:
            S.add(eng, lambda e: e.tensor_scalar(out, in0, s1, s2, op0, op1), reads=reads, writes=writes)

    def cp(eng, out, in_, reads, writes):
        S.add(eng, lambda e: e.tensor_copy(out, in_), reads=reads, writes=writes)

    def tr(out, in_, reads, writes):
        S.add("pe", lambda e: e.transpose(out, in_, ident_sb[:]), reads=reads + ["ident"], writes=writes)

    def scan(out, d0, d1, init, reads, writes):
        S.add("dve", lambda e: e.tensor_tensor_scan(out, d0, d1, init, ALU.mult, ALU.add), reads=reads, writes=writes)

    def mset(ap, val, writes):
        S.add("pool", lambda e: e.memset(ap, val), writes=writes)

    dma("sp", ident_sb[:], ident_d, [], ["ident"], "ident")
    dma("sp", cv_sb[:], cv_d, [], ["cv"], "cv")
    mset(ones_bf[:], 1.0, ["ones"])

    with ExitStack() as ph:
        sc_f = sb("sc_f", [128, 2, 8], F32, ph)
        wm_bf = sb("wm_bf", [128, 8, 6 * D], BF16, ph)
        sp_t = sb("sp_t", [128, 32], F32, ph)
        act(sc_f[:, 0, :], cv_sb[:, CVL["c"]:CVL["c"] + 8], AF.Silu, ["cv"], ["sc_f"])
        act(sc_f[:, 1, :], cv_sb[:, CVL["cctx"]:CVL["cctx"] + 8], AF.Silu, ["cv"], ["sc_f"])
        for s in range(2):
            cp("dve", sc_bf[:, :, s], sc_f[:, s, :], ["sc_f"], ["sc_bf"])
        a0 = CVL["lrua"]
        act(sp_t[:], cv_sb[:, a0:a0 + 32], AF.Exp, ["cv"], ["sp_t"], scale=-1.0)
        act(sp_t[:], sp_t[:], AF.Ln, ["sp_t"], ["sp_t"], bias=1.0)
        tsc("dve", la_sb[:, 0, :], sp_t[:], -8.0, None, ALU.mult, None, ["sp_t"], ["la"])
        tsc("dve", la_sb[:, 1, :], sp_t[:], -16.0, None, ALU.mult, None, ["sp_t"], ["la"])
        def mod_finish(l, bank):
            psv = ps[bank][:, 0:96].rearrange("p (o s) -> p o s", s=2)
            b0 = CVL[("bmod", l)]
            for s_ in range(2):
                tt("dve", modsb[:, l, s_, :], psv[:, :, s_], cv_sb[:, b0:b0 + 48], ALU.add, [pk(bank), "cv"], ["modsb"])
            for s_ in range(2):
                for ni in range(2):
                    sc_off = 8 if ni == 0 else 32
                    n0 = CVL[("norm1", l)] if ni == 0 else CVL[("norm2", l)]
                    stt(gains[:, l, s_, ni, :], modsb[:, l, s_, sc_off:sc_off + 8], 1.0, cv_sb[:, n0:n0 + 8],
                        ALU.add, ALU.mult, ["modsb", "cv"], ["gains"])

        for l in range(1):
            wsrc = wmod_d[l].rearrange("(kc p) n -> p kc n", p=128)
            for part in range(4):
                cs_ = slice(part * 1536, (part + 1) * 1536)
                dma("pool", wm_bf[:, :, cs_], wsrc[:, :, cs_], [], ["wm"], "wm")
            for oc in range(48):
                for kc in range(8):
                    mm(ps[0][:, oc * 2:oc * 2 + 2], wm_bf[:, kc, oc * 128:(oc + 1) * 128], sc_bf[:, kc, :],
                       kc == 0, kc == 7, ["wm", "sc_bf"], [pk(0)])
            mod_finish(l, 0)
    S.barrier()

    def mod_col(l, s, which, c):
        return modsb[:, l, s, which * 8 + c:which * 8 + c + 1]

    def gain_col(l, s, ni, c):
        return gains[:, l, s, ni, c:c + 1]

    def norm_pre(xt, xkey, sq, sqkey):
        for hf in range(2):
            act(sq[:, hf * 4:(hf + 1) * 4, :], xt[:, hf * 4:(hf + 1) * 4, :], AF.Square, [xkey], [sqkey])

    def norm_mid(sq, sqkey, lnm):
        for c in range(8):
            mm(ps[0][:], ones_bf[:], sq[:, c, :], c == 0, c == 7, ["ones", sqkey], [pk(0)])
        act(lnm[:], ps[0][:], AF.Ln, [pk(0)], ["lnm"], bias=EPS, scale=1.0 / D)
        act(ps[1][:], lnm[:], AF.Exp, ["lnm"], [pk(1)], scale=-0.5)

    def norm_post(xt, xkey, tmp2, tmpkey, out, outkey, gain_fn, shift_fn):
        for c in range(8):
            if shift_fn is None:
                stt(out[:, c, :], xt[:, c, :], gain_fn(c), ps[1][:], ALU.mult, ALU.mult,
                    [xkey, pk(1), "gains", "cv"], [outkey])
            else:
                tk = (tmpkey, c % 2)
                stt(tmp2[:, c % 2, :], xt[:, c, :], gain_fn(c), ps[1][:], ALU.mult, ALU.mult,
                    [xkey, pk(1), "gains"], [tk])
                act(out[:, c, :], tmp2[:, c % 2, :], AF.Identity, [tk, "modsb"], [outkey], bias=shift_fn(c))

    def norm_tile(xt, xkey, sq, sqkey, tmp2, tmpkey, out, outkey, gain_fn, shift_fn, lnm):
        norm_pre(xt, xkey, sq, sqkey)
        norm_mid(sq, sqkey, lnm)
        norm_post(xt, xkey, tmp2, tmpkey, out, outkey, gain_fn, shift_fn)

    with ExitStack() as ph:
        xin = [sb(f"xin{i}", [128, 4, D], F32, ph) for i in range(2)]
        xst = [sb(f"xst{i}", [128, 8, TS], F32, ph) for i in range(2)]

        def p0_load(j):
            sl = j % 2
            if j < 8:
                src = xs_d[j * TS:(j + 1) * TS, :].rearrange("(s p) f -> p s f", p=128)
            else:
                src = xp_d.rearrange("(s p) f -> p s f", p=128)
            dma("sp", xin[sl][:], src, [], [("xin", sl)], f"xin{sl}")

        p0_load(0)
        for j in range(NT):
            sl = j % 2
            if j + 1 < NT:
                p0_load(j + 1)
            for c in range(8):
                for s in range(4):
                    tr(ps[c][:, s * 128:(s + 1) * 128], xin[sl][:, s, c * 128:(c + 1) * 128], [("xin", sl)], [pk(c)])
                if c % 2 == 0:
                    act(xst[sl][:, c, :], ps[c][:], AF.Copy, [pk(c)], [("xst", sl)])
                else:
                    cp("dve", xst[sl][:, c, :], ps[c][:], [pk(c)], [("xst", sl)])
            dma("sp", xT_v[:, :, j * TS:(j + 1) * TS], xst[sl][:], [("xst", sl)], [("xT", j)], f"xst{sl}")
    S.barrier()
    nl_run = 0 if stop_after == "p0" else nlayers

    for l in range(nl_run):
        with ExitStack() as ph:
            w_in = sb("w_in", [128, 8, INC], BF16, ph)
            xt = [sb(f"a_xt{i}", [128, 8, TS], F32, ph) for i in range(2)]
            tmp2 = sb("a_tmp", [128, 2, TS], F32, ph)
            hTs = [sb(f"a_hT{i}", [128, 8, TS], BF16, ph) for i in range(2)]
            lnm = sb("a_lnm", [128, TS], F32, ph)
            qk = [sb(f"a_qk{i}", [128, 8, TS], BF16, ph) for i in range(2)]
            vst = [sb(f"a_vst{i}", [128, 4, 1024], BF16, ph) for i in range(2)]
            for i in range(2):
                mset(vst[i][:], 1.0, [("a_vst", i)])
            xy = [sb(f"a_xy{i}", [128, 8, TS], F32, ph) for i in range(2)]
            kvf = sb("a_kvf", [128, 2, 4, 512], F32, ph)
            wsrc = win_d[l].rearrange("(kc p) n -> p kc n", p=128)
            for part in (0, 1, 3, 2):
                cs_ = slice(part * 640, (part + 1) * 640)
                dma("pool", w_in[:, :, cs_], wsrc[:, :, cs_], [], [("w_in", part)], f"w_in{part}")

            def wkeys(c0, c1):
                return [("w_in", p_) for p_ in range(c0 // 640, (c1 - 1) // 640 + 1)]

            def a_load(j):
                sl = j % 2
                dma("sp", xt[sl][:], xT_v[:, :, j * TS:(j + 1) * TS], [("xT", j)], [("a_xt", sl)], f"a_xt{sl}")

            a_load(0)
            a_load(1)
            bank = [0]

            def nxt():
                b = 2 + bank[0] % 5
                bank[0] += 1
                return b

            do_mod = (l + 1 < nlayers)
            if do_mod:
                wmc = [sb(f"a_wmc{i}", [128, 8, 512], BF16, ph) for i in range(2)]
                wmsrc = wmod_d[l + 1].rearrange("(kc p) n -> p kc n", p=128)
            mctr = [0]

            def mod_chunk_load(ck):
                slot = mctr[0] % 2
                mctr[0] += 1
                dma("pool", wmc[slot][:], wmsrc[:, :, ck * 512:(ck + 1) * 512], [], [("a_wmc", slot)], f"a_wmc{slot}")
                return slot

            def mod_chunk_mm(ck, slot):
                for ol in range(4):
                    oc = ck * 4 + ol
                    for kc in range(8):
                        mm(ps[7][:, oc * 2:oc * 2 + 2], wmc[slot][:, kc, ol * 128:(ol + 1) * 128], sc_bf[:, kc, :],
                           kc == 0, kc == 7, [("a_wmc", slot), "sc_bf"], [pk(7)])

            def a_norm(j, part):
                sl_ = j % 2
                st_ = 1 if j == 8 else 0
                hk = ("a_hT", sl_)
                if part == 0:
                    norm_pre(xt[sl_], ("a_xt", sl_), hTs[sl_], hk)
                else:
                    norm_mid(hTs[sl_], hk, lnm)
                    norm_post(xt[sl_], ("a_xt", sl_), tmp2, "a_tmp", hTs[sl_], hk,
                              lambda c, st_=st_, l=l: gain_col(l, st_, 0, c), lambda c, st_=st_, l=l: mod_col(l, st_, 0, c))

            a_norm(0, 0)
            a_norm(0, 1)
            for j in range(NT):
                sl = j % 2
                st = 1 if j == 8 else 0
                t0 = j * TS
                hT = hTs[sl]
                hkey = ("a_hT", sl)
                if j + 1 < NT:
                    a_norm(j + 1, 0)
                gi = 0
                pend = []
                if do_mod:
                    for ck in ([j] + ([9 + j] if j < 3 else [])):
                        pend.append((ck, mod_chunk_load(ck)))
                for n in list(range(0, 8)) + list(range(12, 20)):
                    if gi == 5 and j + 1 < NT:
                        a_norm(j + 1, 1)
                    if gi == 11:
                        for ck, slot in pend:
                            mod_chunk_mm(ck, slot)
                    gi += 1
                    b = nxt()
                    for kc in range(8):
                        mm(ps[b][:], w_in[:, kc, n * 128:(n + 1) * 128], hT[:, kc, :], kc == 0, kc == 7,
                           wkeys(n * 128, (n + 1) * 128) + [hkey], [pk(b)])
                    if n < 4:
                        act(qk[sl][:, n, :], ps[b][:], AF.Copy, [pk(b)], [("a_qk", sl)], scale=0.125)
                    elif n < 8:
                        cp("dve", qk[sl][:, n, :], ps[b][:], [pk(b)], [("a_qk", sl)])
                    elif n < 16:
                        cp("dve", xy[sl][:, n - 12, :], ps[b][:], [pk(b)], [("a_xy", sl)])
                    else:
                        act(xy[sl][:, n - 12, :], ps[b][:], AF.Gelu_apprx_tanh, [pk(b)], [("a_xy", sl)])
                import os as _os
                J8 = (j == 8) and not _os.environ.get('SKIP8')
                for which in ([1, 0] if J8 else [1]):
                    col0 = 1024 if which == 1 else 512
                    for s in range(4):
                        b = nxt()
                        for kc in range(8):
                            mm(ps[b][:], hT[:, kc, s * 128:(s + 1) * 128], w_in[:, kc, col0:col0 + 512],
                               kc == 0, kc == 7, wkeys(col0, col0 + 512) + [hkey], [pk(b)])
                        if J8:
                            if s % 2 == 1:
                                act(kvf[:, which, s, :], ps[b][:], AF.Copy, [pk(b)], ["a_kvf"])
                            else:
                                cp("dve", kvf[:, which, s, :], ps[b][:], [pk(b)], ["a_kvf"])
                            if which == 1:
                                kv5 = kvf[:, which, s, :].rearrange("p (h2 par d) -> p h2 par d", h2=4, par=2)
                                v5 = vst[sl][:, s, :].rearrange("p (h2 par x d) -> p h2 par x d", h2=4, par=2, x=2)
                                for par in range(2):
                                    cp("pool", v5[:, :, par, par, :], kv5[:, :, par, :], ["a_kvf", ("a_vst", sl)], [("a_vst", sl)])
                        elif which == 1:
                            p5 = ps[b][:].rearrange("p (h2 par d) -> p h2 par d", h2=4, par=2)
                            v5 = vst[sl][:, s, :].rearrange("p (h2 par x d) -> p h2 par x d", h2=4, par=2, x=2)
                            for par in range(2):
                                if (s + par) % 2 == 0:
                                    act(v5[:, :, par, par, :], p5[:, :, par, :], AF.Copy, [pk(b), ("a_vst", sl)], [("a_vst", sl)])
                                else:
                                    cp("dve", v5[:, :, par, par, :], p5[:, :, par, :], [pk(b), ("a_vst", sl)], [("a_vst", sl)])
                dma("sp", qT_v[:, :, t0:t0 + TS], qk[sl][:, 0:4, :], [("a_qk", sl)], [], f"a_qk{sl}")
                dma("sp", kT_v[:, :, t0:t0 + TS], qk[sl][:, 4:8, :], [("a_qk", sl)], [], f"a_qk{sl}")
                dma("sp", vt_d[t0:t0 + TS, :].rearrange("(s p) f -> p s f", p=128), vst[sl][:],
                    [("a_vst", sl)], [], f"a_vst{sl}")
                dma("sp", xbT_v[:, :, t0:t0 + TS], xy[sl][:, 0:4, :], [("a_xy", sl)], [], f"a_xy{sl}")
                dma("sp", gyT_v[:, :, t0:t0 + TS], xy[sl][:, 4:8, :], [("a_xy", sl)], [], f"a_xy{sl}")
                if j + 2 < NT:
                    a_load(j + 2)
                if J8:
                    for s_ in range(2):
                        dma("sp", nk_d[s_, l].rearrange("(h p) f -> p h f", p=128), kvf[:, 0, 2 * s_:2 * s_ + 2, :], ["a_kvf"], [], "a_kvf")
                        dma("sp", nv_d[s_, l].rearrange("(h p) f -> p h f", p=128), kvf[:, 1, 2 * s_:2 * s_ + 2, :], ["a_kvf"], [], "a_kvf")
            if do_mod:
                mod_finish(l + 1, 7)
        S.barrier()
        if stop_after == ("A", l):
            break

        with ExitStack() as ph:
            TI = sb("b_TI", [128, NH, NW, 64], F32, ph)
            TF = sb("b_TF", [128, NH, NW, 64], F32, ph)
            with ExitStack() as ph2:
                Tb = sb("b_Tb", [128, NH, NW, 64], F32, ph2)
                CIs = sb("b_CI", [128, NW, 64], F32, ph2)
                CFs = sb("b_CF", [128, NW, 64], F32, ph2)
                dma("sp", CIs[:], CI_d, [], ["b_CI"], "b_CI")
                dma("sp", CFs[:], CF_d, [], ["b_CF"], "b_CF")
                mset(Tb[:], 0.0, ["b_Tb"])
                src = bass.AP(rpb_d.tensor, l * (NH * 15 * 127) + 63, [[127, NH * 15], [-1, 64], [1, 64]])
                dma("sp", toe_d, src, [], ["toe"], "b_toe")
                for h in range(NH):
                    for a in range(2):
                        dma("sp", Tb[a * 64:(a + 1) * 64, h, 3 + a:18 + a, :],
                            toe_d[h * 15:(h + 1) * 15].rearrange("u k q -> k u q"), ["toe", "b_Tb"], ["b_Tb"], "b_Tb")
                for h in range(NH):
                    tt("dve", TI[:, h], Tb[:, h], CIs[:], ALU.add, ["b_Tb", "b_CI"], ["b_TI"])
                    tt("pool", TF[:, h], Tb[:, h], CFs[:], ALU.add, ["b_Tb", "b_CF"], ["b_TF"])
                S.barrier()
            E0s = sb("b_E0", [128, 6, 512], BF16, ph)
            E7s = sb("b_E7", [128, 6, 512], BF16, ph)
            dma("sp", E0s[:], E0_d, [], ["b_E0"], "b_E0")
            dma("sp", E7s[:], E7_d, [], ["b_E7"], "b_E7")
            kcT = sb("b_kcT", [128, 4, 256], BF16, ph)
            vcw = sb("b_vcw", [128, 2, 1024], BF16, ph)
            with ExitStack() as ph2:
                ckf = sb("b_ckf", [128, 2, 512], F32, ph2)
                cvf = sb("b_cvf", [128, 2, 512], F32, ph2)
                dma("sp", ckf[:], ck_d[l].rearrange("(t p) f -> p t f", p=128), [], ["b_ckf"], "b_ckf")
                dma("sp", cvf[:], cvv_d[l].rearrange("(t p) f -> p t f", p=128), [], ["b_cvf"], "b_cvf")
                mset(vcw[:], 1.0, ["b_vcw"])
                for t in range(2):
                    c5 = cvf[:, t, :].rearrange("p (h2 par d) -> p h2 par d", h2=4, par=2)
                    v5 = vcw[:, t, :].rearrange("p (h2 par x d) -> p h2 par x d", h2=4, par=2, x=2)
                    for par in range(2):
                        cp("dve", v5[:, :, par, par, :], c5[:, :, par, :], ["b_cvf", "b_vcw"], ["b_vcw"])
                for c in range(4):
                    for t in range(2):
                        tr(ps[c][:, t * 128:(t + 1) * 128], ckf[:, t, c * 128:(c + 1) * 128], ["b_ckf"], [pk(c)])
                    cp("dve", kcT[:, c, :], ps[c][:, 0:256], [pk(c)], ["b_kcT"])
                S.barrier()
            qt = [sb(f"b_qt{i}", [128, 4, TS], BF16, ph) for i in range(2)]
            kw = [sb(f"b_kw{i}", [128, 4, 1024], BF16, ph) for i in range(2)]
            vw = [sb(f"b_vw{i}", [128, 8, 1024], BF16, ph) for i in range(2)]
            ST = [sb(f"b_ST{i}", [128, TS], F32, ph) for i in range(3)]
            PT = [sb(f"b_PT{i}", [128, 10, TS], BF16, ph) for i in range(2)]
            lnd = [sb(f"b_lnd{i}", [128, TS], F32, ph) for i in range(1)] * 2
            rden = [sb(f"b_rden{i}", [128, TS], F32, ph) for i in range(1)] * 2
            att = [sb(f"b_att{i}", [128, 4, TS], BF16, ph) for i in range(2)]

            blocks = []
            for j in range(8):
                if j == 0:
                    r_lo, nk_, kind, dl = 0, 6, "E0", 0
                elif j == 7:
                    r_lo, nk_, kind, dl = 52, 6, "E7", -4
                else:
                    r_lo, nk_, kind, dl = 8 * j - 4, 8, "I", -4
                blocks.append(dict(q0=j * TS, n=TS, k0=r_lo * 64, nk=nk_, kind=kind, delta0=dl, ctx=True, tile=j, qoff=0))
            for s in range(2):
                blocks.append(dict(q0=SEQ_S + s * SEQ_P, n=SEQ_P, k0=SEQ_S + s * SEQ_P, nk=2, kind="N", delta0=0,
                                   ctx=False, tile=8, qoff=s * SEQ_P))

            def b_load(bi):
                bl = blocks[bi]
                sl = bi % 2
                n, nk_ = bl["n"], bl["nk"]
                dma("sp", qt[sl][:, :, 0:n], qT_v[:, :, bl["q0"]:bl["q0"] + n], [], [("b_qt", sl)], f"b_qt{sl}")
                dma("sp", kw[sl][:, :, 0:nk_ * 128], kT_v[:, :, bl["k0"]:bl["k0"] + nk_ * 128], [], [("b_kw", sl)],
                    f"b_kw{sl}")
                dma("sp", vw[sl][:, 0:nk_, :],
                    vt_d[bl["k0"]:bl["k0"] + nk_ * 128, :].rearrange("(t p) f -> p t f", p=128),
                    [], [("b_vw", sl)], f"b_vw{sl}")

            sctr = [0]

            def lhs_v(tensor_ap, t, h):
                return tensor_ap[:, t, h * 128:(h + 1) * 128]

            def qrange(bl, t):
                kind = bl["kind"]
                if kind == "N" or t >= bl["nk"]:
                    return 0, bl["n"]
                ok = []
                for b_ in range(8):
                    v = False
                    for a_ in range(2):
                        if kind == "I":
                            v = v or (3 <= 2 * t + 3 + a_ - b_ <= 10)
                        else:
                            qr = b_ if kind == "E0" else 56 + b_
                            kr = (2 * t + a_) if kind == "E0" else (52 + 2 * t + a_)
                            rs = min(max(qr - 4, 0), 56)
                            v = v or (rs <= kr < rs + 8)
                    if v:
                        ok.append(b_)
                return ok[0] * 64, (ok[-1] + 1) * 64

            def do_scores(bi, h, hs):
                bl = blocks[bi]
                sl = bi % 2
                n, nk_ = bl["n"], bl["nk"]
                c, po = h // 2, 64 * (h % 2)
                ntl = nk_ + (2 if bl["ctx"] else 0)
                for t in range(ntl):
                    b = 2 + sctr[0] % 4
                    sti = sctr[0] % 3
                    sctr[0] += 1
                    if t < nk_:
                        lhs = kw[sl][po:po + 64, c, t * 128:(t + 1) * 128]
                        rk = [("b_kw", sl)]
                    else:
                        tc_ = t - nk_
                        lhs = kcT[po:po + 64, c, tc_ * 128:(tc_ + 1) * 128]
                        rk = ["b_kcT"]
                    c0, c1 = qrange(bl, t)
                    mm(ps[b][:, c0:c1], lhs, qt[sl][po:po + 64, c, c0:c1], True, True, rk + [("b_qt", sl)], [pk(b)])
                    if t < nk_ and bl["kind"] != "N":
                        w0 = 10 - (bl["delta0"] + 2 * t)
                        tab = TI if bl["kind"] == "I" else TF
                        tv = tab[:, h, w0 + c0 // 64:w0 + c1 // 64, :].rearrange("p w q -> p (w q)")
                        tt("dve", ST[sti][:, c0:c1], ps[b][:, c0:c1], tv, ALU.add, [pk(b), "b_TI", "b_TF"], [("b_ST", sti)])
                        if bl["kind"] != "I":
                            Es = E0s if bl["kind"] == "E0" else E7s
                            tt("pool", ST[sti][:, c0:c1], ST[sti][:, c0:c1], Es[:, t, c0:c1], ALU.add,
                               [("b_ST", sti), "b_E0", "b_E7"], [("b_ST", sti)])
                        act(PT[hs][:, t, c0:c1], ST[sti][:, c0:c1], AF.Exp, [("b_ST", sti)], [("b_PT", hs)])
                    else:
                        act(PT[hs][:, t, c0:c1], ps[b][:, c0:c1], AF.Exp, [pk(b)], [("b_PT", hs)])

            def do_pv(bi, h, hs):
                bl = blocks[bi]
                sl = bi % 2
                asl = bl["tile"] % 2
                n, nk_ = bl["n"], bl["nk"]
                c, po = h // 2, 64 * (h % 2)
                pd = 64 - po
                ntl = nk_ + (2 if bl["ctx"] else 0)
                b = 6 + hs
                torder = list(range(nk_, ntl)) + list(range(nk_))
                for ti, t in enumerate(torder):
                    if t < nk_:
                        lhs = lhs_v(vw[sl], t, h)
                        rk = [("b_vw", sl)]
                    else:
                        lhs = lhs_v(vcw, t - nk_, h)
                        rk = ["b_vcw"]
                    c0, c1 = qrange(bl, t)
                    mm(ps[b][:, c0:c1], lhs, PT[hs][:, t, c0:c1], ti == 0, ti == ntl - 1, rk + [("b_PT", hs)], [pk(b)])
                act(lnd[hs][po:po + 64, 0:n], ps[b][pd:pd + 64, 0:n], AF.Ln, [pk(b)], ["b_lnd"])
                act(rden[hs][po:po + 64, 0:n], lnd[hs][po:po + 64, 0:n], AF.Exp, ["b_lnd"], ["b_rden"],
                    scale=-1.0)
                qo = bl["qoff"]
                tt("dve", att[asl][po:po + 64, c, qo:qo + n], ps[b][po:po + 64, 0:n], rden[hs][po:po + 64, 0:n],
                   ALU.mult, [pk(b), "b_rden"], [("b_att", asl)])

            b_load(0)
            b_load(1)
            units = [(bi, h) for bi in range(len(blocks)) for h in range(NH)]
            do_scores(units[0][0], units[0][1], 0)
            for ui, (bi, h) in enumerate(units):
                bl = blocks[bi]
                if ui + 1 < len(units):
                    nbi, nh = units[ui + 1]
                    do_scores(nbi, nh, (ui + 1) % 2)
                do_pv(bi, h, ui % 2)
                if h == NH - 1:
                    if bi + 2 < len(blocks):
                        b_load(bi + 2)
                    last_of_tile = (bi + 1 == len(blocks)) or (blocks[bi + 1]["tile"] != bl["tile"])
                    if last_of_tile:
                        tj = bl["tile"]
                        dma("sp", mixT_v[:, 0:4, tj * TS:(tj + 1) * TS], att[tj % 2][:], [("b_att", tj % 2)], [],
                            f"b_att{tj % 2}")
        S.barrier()
        if stop_after == ("B", l):
            break

        phw = ExitStack()
        wg = sb("w_g", [128, 8, DFF], BF16, phw)
        with ExitStack() as ph:
            wbd = sb("c_wbd", [128, 2, 2, 4, 128], BF16, ph)
            xbh = [sb(f"c_xbh{i}", [128, 4, TS + 4], F32, ph) for i in range(2)]
            xc = [sb(f"c_xc{i}", [128, 4, TS], F32, ph) for i in range(2)]
            gy = xbh
            xcb = [sb(f"c_xcb{i}", [128, 4, TS], BF16, ph) for i in range(2)]
            rr = [sb(f"c_r{i}", [128, 4, TS], F32, ph) for i in range(2)]
            ii = [sb(f"c_i{i}", [128, 4, TS], F32, ph) for i in range(2)]
            aa = [sb(f"c_a{i}", [128, 4, TS], F32, ph) for i in range(2)]
            ss = [sb(f"c_s{i}", [128, 4, TS], F32, ph) for i in range(2)]
            hfb = [sb(f"c_hf{i}", [128, 4, TS], F32, ph) for i in range(2)]
            hb = [sb(f"c_hb{i}", [128, 4, TS], F32, ph) for i in range(2)]
            rec = [sb(f"c_rec{i}", [128, 4, TS], BF16, ph) for i in range(2)]
            mset(wbd[:], 0.0, ["c_wbd"])
            for d in range(2):
                for g, wsrc_d in enumerate((wr_d, wi_d)):
                    for par in range(2):
                        src = wsrc_d[l, d, par::2].rearrange("n j k -> j n k")
                        dma("pool", wbd[par * 64:(par + 1) * 64, d, g, :, par * 64:(par + 1) * 64], src,
                            ["c_wbd"], ["c_wbd"], "c_wbd")
            segs = [dict(t0=j * TS, n=TS, left=(j > 0), right=(j < 7), sample=True, idx=j) for j in range(8)]
            segs += [dict(t0=SEQ_S + s * SEQ_P, n=SEQ_P, left=False, right=False, sample=False, idx=s) for s in range(2)]
            gctr = [0]

            def gates(d, xcs, xckey, n, p, imul=True):
                cp("pool", xcb[p][:, :, 0:n], xcs[:, :, 0:n], [xckey], [("c_xcb", p)])
                for c in range(4):
                    for g in range(2):
                        b = gctr[0] % 8
                        gctr[0] += 1
                        mm(ps[b][:, 0:n], wbd[:, d, g, c, :], xcb[p][:, c, 0:n], True, True, ["c_wbd", ("c_xcb", p)], [pk(b)])
                        dst = rr[p] if g == 0 else ii[p]
                        bname = ("br", l, d) if g == 0 else ("bi", l, d)
                        act(dst[:, c, 0:n], ps[b][:, 0:n], AF.Sigmoid, [pk(b), "cv"], [("c_r", p) if g == 0 else ("c_i", p)],
                            bias=cvc(bname, c))
                lcol = (l * 2 + d) * 4
                for c in range(4):
                    act(aa[p][:, c, 0:n], rr[p][:, c, 0:n], AF.Exp, [("c_r", p), "la"], [("c_a", p)],
                        scale=la_sb[:, 0, lcol + c:lcol + c + 1])
                    act(ss[p][:, c, 0:n], rr[p][:, c, 0:n], AF.Exp, [("c_r", p), "la"], [("c_s", p)],
                        scale=la_sb[:, 1, lcol + c:lcol + c + 1])
                for c in range(4):
                    act(ss[p][:, c, 0:n], ss[p][:, c, 0:n], AF.Sqrt, [("c_s", p)], [("c_s", p)], bias=1.0, scale=-1.0)
                if imul:
                    gates_imul(xcs, xckey, n, p)

            def gates_imul(xcs, xckey, n, p):
                for c in range(4):
                    tt("pool", ii[p][:, c, 0:n], ii[p][:, c, 0:n], xcs[:, c, 0:n], ALU.mult, [("c_i", p), xckey], [("c_i", p)])

            def gates_tail(xcs, xckey, n, p):
                for c in range(4):
                    tt("dve", ss[p][:, c, 0:n], ss[p][:, c, 0:n], ii[p][:, c, 0:n], ALU.mult, [("c_s", p), ("c_i", p)], [("c_s", p)])

            def cf_load(si):
                sg = segs[si]
                sl = si % 2
                n, t0 = sg["n"], sg["t0"]
                lo = 0 if sg["left"] else 2
                hi = n + 3 if sg["right"] else n + 2
                if not sg["left"]:
                    mset(xbh[sl][:, :, 0:2], 0.0, [("c_xbh", sl)])
                if not sg["right"]:
                    mset(xbh[sl][:, :, n + 2:n + 3], 0.0, [("c_xbh", sl)])
                dma("sp", xbh[sl][:, :, lo:hi], xbT_v[:, :, t0 - 2 + lo:t0 - 2 + hi], [("c_xbh", sl)], [("c_xbh", sl)],
                    f"c_xbh{sl}")

            def f_stageA(si):
                sg = segs[si]
                sl = si % 2
                n, t0 = sg["n"], sg["t0"]
                if si + 1 < len(segs):
                    cf_load(si + 1)
                xk = ("c_xc", sl)
                for c in range(4):
                    act(xc[sl][:, c, 0:n], xbh[sl][:, c, 0:n], AF.Identity, [("c_xbh", sl), "cv"], [xk],
                        bias=cvc(("convb", l), c), scale=cvc(("convw", l, 0), c))
                for jj in range(1, 4):
                    for c in range(4):
                        stt(xc[sl][:, c, 0:n], xbh[sl][:, c, jj:jj + n], cvc(("convw", l, jj), c), xc[sl][:, c, 0:n],
                            ALU.mult, ALU.add, [("c_xbh", sl), xk, "cv"], [xk])
                dma("sp", xcT_v[:, :, t0:t0 + n], xc[sl][:, :, 0:n], [xk], [], f"c_xc{sl}")
                gates(0, xc[sl], xk, n, sl)

            def f_stageB(si):
                sg = segs[si]
                sl = si % 2
                n, t0 = sg["n"], sg["t0"]
                gates_tail(xc[sl], ("c_xc", sl), n, sl)
                for c in range(4):
                    if sg["sample"] and sg["idx"] > 0:
                        init = hfb[1 - sl][:, c, TS - 1:TS]
                    elif sg["sample"]:
                        init = cvc(("h0", l, 0), c)
                    else:
                        init = 0.0
                    scan(hfb[sl][:, c, 0:n], aa[sl][:, c, 0:n], ss[sl][:, c, 0:n], init,
                         [("c_a", sl), ("c_s", sl), ("c_hf", 1 - sl), "cv"], [("c_hf", sl)])
                dma("sp", hfT_v[:, :, t0:t0 + n], hfb[sl][:, :, 0:n], [("c_hf", sl)], [], f"c_hfo{sl}")
                if not sg["sample"]:
                    dma("sp", ns_d[sg["idx"], l, 0].rearrange("(c p o) -> p c o", p=128, o=1),
                        hfb[sl][:, :, n - 1:n], [("c_hf", sl)], [], f"c_nsf{sl}", slow=True)

            cf_load(0)
            f_stageA(0)
            for si in range(len(segs)):
                if si + 1 < len(segs):
                    f_stageA(si + 1)
                f_stageB(si)
            S.barrier()
            order = list(range(7, -1, -1)) + [8, 9]
            wsrc = wg_d[l].rearrange("(kc p) n -> p kc n", p=128)
            for part in range(2):
                cs_ = slice(part * 1408, (part + 1) * 1408)
                dma("pool", wg[:, :, cs_], wsrc[:, :, cs_], [], ["w_g"], "w_g")

            def cb_load(oi):
                sg = segs[order[oi]]
                sl = oi % 2
                n, t0 = sg["n"], sg["t0"]
                dma("sp", xc[sl][:, :, 0:n], xcT_v[:, :, t0:t0 + n], [], [("c_xc", sl)], f"c_xc{sl}")
                dma("sp", gy[sl][:, :, 0:n], gyT_v[:, :, t0:t0 + n], [], [("c_gy", sl)], f"c_gy{sl}")
                dma("sp", hfb[sl][:, :, 0:n], hfT_v[:, :, t0:t0 + n], [], [("c_hf", sl)], f"c_hfi{sl}")

            def b_stageA(oi):
                sg = segs[order[oi]]
                sl = oi % 2
                gates(1, xc[sl], ("c_xc", sl), sg["n"], sl, imul=(oi == 0))

            def b_stageB(oi):
                sg = segs[order[oi]]
                sl = oi % 2
                n, t0 = sg["n"], sg["t0"]
                gates_tail(xc[sl], ("c_xc", sl), n, sl)
                for c in range(4):
                    if sg["sample"] and sg["idx"] < 7:
                        init = hb[1 - sl][:, c, 0:1]
                    elif sg["sample"]:
                        init = cvc(("h0", l, 1), c)
                    else:
                        init = 0.0
                    scan(hb[sl][:, c, 0:n][:, ::-1], aa[sl][:, c, 0:n][:, ::-1], ss[sl][:, c, 0:n][:, ::-1], init,
                         [("c_a", sl), ("c_s", sl), ("c_hb", 1 - sl), "cv"], [("c_hb", sl)])
                if not sg["sample"]:
                    dma("sp", ns_d[sg["idx"], l, 1].rearrange("(c p o) -> p c o", p=128, o=1),
                        hb[sl][:, :, 0:1], [("c_hb", sl)], [], f"c_nsb{sl}", slow=True)
                for c in range(4):
                    tt("pool", hfb[sl][:, c, 0:n], hfb[sl][:, c, 0:n], hb[sl][:, c, 0:n], ALU.add,
                       [("c_hf", sl), ("c_hb", sl)], [("c_hf", sl)])
                    tt("dve", rec[sl][:, c, 0:n], hfb[sl][:, c, 0:n], gy[sl][:, c, 0:n], ALU.mult,
                       [("c_hf", sl), ("c_gy", sl)], [("c_rec", sl)])
                dma("sp", mixT_v[:, 4:8, t0:t0 + n], rec[sl][:, :, 0:n], [("c_rec", sl)], [], f"c_rec{sl}")

            cb_load(0)
            cb_load(1)
            b_stageA(0)
            for oi in range(len(order)):
                if oi + 1 < len(order):
                    b_stageA(oi + 1)
                b_stageB(oi)
                if oi + 1 < len(order):
                    sg1 = segs[order[oi + 1]]
                    gates_imul(xc[(oi + 1) % 2], ("c_xc", (oi + 1) % 2), sg1["n"], (oi + 1) % 2)
                if oi + 2 < len(order):
                    cb_load(oi + 2)
        S.barrier()
        if stop_after == ("C", l):
            phw.close()
            break

        wu = sb("w_u", [128, 8, DFF], BF16, phw)
        wd = sb("w_d", [128, NFF, D], BF16, phw)
        with ExitStack() as ph:
            w_o = sb("w_o", [128, 8, D], BF16, ph)
            xt = [sb(f"d_xt{i}", [128, 8, TS], F32, ph) for i in range(2)]
            mx = [sb(f"d_mx{i}", [128, 8, TS], BF16, ph) for i in range(2)]
            dma("pool", w_o[:], wout_d[l].rearrange("(kc p) n -> p kc n", p=128), [], ["w_o"], "w_o")
            wsrc = wu_d[l].rearrange("(kc p) n -> p kc n", p=128)
            for part in range(2):
                cs_ = slice(part * 1408, (part + 1) * 1408)
                dma("pool", wu[:, :, cs_], wsrc[:, :, cs_], [], ["w_u"], "w_u")

            def d1_load(j):
                sl = j % 2
                dma("sp", xt[sl][:], xT_v[:, :, j * TS:(j + 1) * TS], [("xT", j)], [("d_xt", sl)], f"d_xt{sl}")
                dma("sp", mx[sl][:], mixT_v[:, :, j * TS:(j + 1) * TS], [], [("d_mx", sl)], f"d_mx{sl}")

            d1_load(0)
            for j in range(NT):
                sl = j % 2
                st = 1 if j == 8 else 0
                if j + 1 < NT:
                    d1_load(j + 1)
                for n_ in range(8):
                    b = n_ % 4
                    for kc in range(8):
                        mm(ps[b][:], w_o[:, kc, n_ * 128:(n_ + 1) * 128], mx[sl][:, kc, :], kc == 0, kc == 7,
                           ["w_o", ("d_mx", sl)], [pk(b)])
                    stt(xt[sl][:, n_, :], ps[b][:], mod_col(l, st, 2, n_), xt[sl][:, n_, :], ALU.mult, ALU.add,
                        [pk(b), ("d_xt", sl), "modsb"], [("d_xt", sl)])
                dma("sp", xT_v[:, :, j * TS:(j + 1) * TS], xt[sl][:], [("d_xt", sl)], [("xT", j)], f"d_xo{sl}")
        S.barrier()
        if stop_after == ("D1", l):
            phw.close()
            break

        with ExitStack() as ph:
            xt = [sb(f"e_xt{i}", [128, 8, TS], F32, ph) for i in range(2)]
            hT = sb("e_hT", [128, 8, TS], BF16, ph)
            lnm = sb("e_lnm", [128, TS], F32, ph)
            sgt = sb("e_sg", [128, 2, TS], F32, ph)
            actb = sb("e_act", [128, NFF, TS], BF16, ph)
            wsrc = wd_d[l].rearrange("(f p) n -> p f n", p=128)
            for part in range(2):
                dma("pool", wd[:, part * 11:(part + 1) * 11, :], wsrc[:, part * 11:(part + 1) * 11, :], [], ["w_d"], "w_d")
            def d2_load(j):
                sl = j % 2
                dma("sp", xt[sl][:], xT_v[:, :, j * TS:(j + 1) * TS], [("xT", j)], [("e_xt", sl)], f"e_xt{sl}")

            d2_load(0)

            def e_norm(j, part):
                sl_ = j % 2
                st_ = 1 if j == 8 else 0
                if part == 0:
                    norm_pre(xt[sl_], ("e_xt", sl_), hT, "e_hT")
                else:
                    norm_mid(hT, "e_hT", lnm)
                    norm_post(xt[sl_], ("e_xt", sl_), sgt, "e_sg", hT, "e_hT",
                              lambda c, st_=st_, l=l: gain_col(l, st_, 1, c), lambda c, st_=st_, l=l: mod_col(l, st_, 3, c))

            if NT > 1:
                d2_load(1)
            e_norm(0, 0)
            e_norm(0, 1)
            for j in range(NT):
                sl = j % 2
                st = 1 if j == 8 else 0
                for f in range(NFF):
                    bg, bu = 2 + f % 2, 4 + f % 2
                    for kc in range(8):
                        mm(ps[bg][:], wg[:, kc, f * 128:(f + 1) * 128], hT[:, kc, :], kc == 0, kc == 7, ["w_g", "e_hT"], [pk(bg)])
                    for kc in range(8):
                        mm(ps[bu][:], wu[:, kc, f * 128:(f + 1) * 128], hT[:, kc, :], kc == 0, kc == 7, ["w_u", "e_hT"], [pk(bu)])
                    sk = ("e_sg", f % 2)
                    act(sgt[:, f % 2, :], ps[bg][:], AF.Silu, [pk(bg)], [sk])
                    tt("dve", actb[:, f, :], sgt[:, f % 2, :], ps[bu][:], ALU.mult, [sk, pk(bu)], ["e_act"])
                if j + 1 < NT:
                    e_norm(j + 1, 0)
                for n_ in range(8):
                    if n_ == 2 and j + 1 < NT:
                        e_norm(j + 1, 1)
                    b = 6 + n_ % 2
                    for f in range(NFF):
                        mm(ps[b][:], wd[:, f, n_ * 128:(n_ + 1) * 128], actb[:, f, :], f == 0, f == NFF - 1,
                           ["w_d", "e_act"], [pk(b)])
                    stt(xt[sl][:, n_, :], ps[b][:], mod_col(l, st, 5, n_), xt[sl][:, n_, :], ALU.mult, ALU.add,
                        [pk(b), ("e_xt", sl), "modsb"], [("e_xt", sl)])
                dma("sp", xT_v[:, :, j * TS:(j + 1) * TS], xt[sl][:], [("e_xt", sl)], [("xT", j)], f"e_xo{sl}")
                if j + 2 < NT:
                    d2_load(j + 2)
        S.barrier()
        phw.close()

    if stop_after is None:
        with ExitStack() as ph:
            xt = [sb(f"f_xt{i}", [128, 8, TS], F32, ph) for i in range(2)]
            sq = sb("f_sq", [128, 8, TS], BF16, ph)
            yT = sb("f_yT", [128, 8, TS], F32, ph)
            lnm = sb("f_lnm", [128, TS], F32, ph)
            yo = [sb(f"f_yo{i}", [128, 4, D], F32, ph) for i in range(2)]

            def f_load(j):
                sl = j % 2
                dma("sp", xt[sl][:], xT_v[:, :, j * TS:(j + 1) * TS], [("xT", j)], [("f_xt", sl)], f"f_xt{sl}")

            f_load(0)
            nf0 = CVL["normf"]
            for j in range(NT):
                sl = j % 2
                if j + 1 < NT:
                    f_load(j + 1)
                norm_tile(xt[sl], ("f_xt", sl), sq, "f_sq", None, None, yT, "f_yT",
                          lambda c: cv_sb[:, nf0 + c:nf0 + c + 1], None, lnm)
                k = 0
                for s in range(4):
                    for half in range(2):
                        b = 2 + k % 6
                        k += 1
                        for cc in range(4):
                            c = half * 4 + cc
                            tr(ps[b][:, cc * 128:(cc + 1) * 128], yT[:, c, s * 128:(s + 1) * 128], ["f_yT"], [pk(b)])
                        if half == 0:
                            act(yo[sl][:, s, 0:512], ps[b][:], AF.Copy, [pk(b)], [("f_yo", sl)])
                        else:
                            cp("dve", yo[sl][:, s, 512:1024], ps[b][:], [pk(b)], [("f_yo", sl)])
                if j < 8:
                    dst = ys_d[j * TS:(j + 1) * TS, :].rearrange("(s p) f -> p s f", p=128)
                else:
                    dst = yp_d.rearrange("(s p) f -> p s f", p=128)
                dma("sp", dst, yo[sl][:], [("f_yo", sl)], [], f"f_yo{sl}")
        S.barrier()

    nsem = S.emit(nc, es)
    es.close()
    return nc, nsem


def make_in_maps(inp, nlayers=L):
    import ml_dtypes
    f = lambda a: np.ascontiguousarray(np.asarray(a, dtype=np.float32))
    ident, CI, CF, E0, E7 = build_consts()
    E0 = E0.astype(ml_dtypes.bfloat16)
    E7 = E7.astype(ml_dtypes.bfloat16)
    rpadh = np.zeros((nlayers, NH, 15, 127), np.float32)
    rpadh[:, :, :, 48:79] = f(inp["rpb"][:nlayers])[:, :, ::-1, ::-1]
    rpadh = np.ascontiguousarray(rpadh.reshape(nlayers, NH * 15, 127))
    shared = {
        "w_mod": f(inp["w_mod"][:nlayers]), "w_in": f(inp["w_in"][:nlayers]), "w_out": f(inp["w_out"][:nlayers]),
        "w_gate": f(inp["w_gate"][:nlayers]), "w_up": f(inp["w_up"][:nlayers]), "w_down": f(inp["w_down"][:nlayers]),
        "rpadh": rpadh,
        "lru_wr": f(inp["lru_wr"][:nlayers]), "lru_wi": f(inp["lru_wi"][:nlayers]),
        "ident": ident, "CI": CI, "CF": CF, "E0": E0, "E7": E7,
    }
    xs, xp = f(inp["x_sample"]), f(inp["x_prompt"])
    ck, cvv = f(inp["cache_k"]), f(inp["cache_v"])
    st, cvec, cctx = f(inp["state_lru"]), f(inp["c"]), f(inp["c_ctx"])
    P = {k: f(inp[k]) for k in ("norm1", "norm2", "b_mod", "conv_w", "conv_b", "lru_a", "lru_br", "lru_bi", "norm_final")}
    maps = []
    for b in range(8):
        rows = np.zeros((NV, 128), np.float32)

        def put(name, vec):
            v = np.asarray(vec, np.float32).reshape(-1, 128)
            rows[CVL[name]:CVL[name] + v.shape[0]] = v

        for l in range(L):
            put(("norm1", l), P["norm1"][l])
            put(("norm2", l), P["norm2"][l])
            put(("bmod", l), P["b_mod"][l])
            for j in range(4):
                put(("convw", l, j), P["conv_w"][l, j])
            put(("convb", l), P["conv_b"][l])
            for d in range(2):
                put(("br", l, d), P["lru_br"][l, d])
                put(("bi", l, d), P["lru_bi"][l, d])
                put(("h0", l, d), st[b, l, d])
        put("lrua", P["lru_a"].reshape(-1))
        put("normf", P["norm_final"])
        put("c", cvec[b])
        put("cctx", cctx)
        m = dict(shared)
        m["cv"] = np.ascontiguousarray(rows.T)
        m["xs"] = xs[b]
        m["xp"] = xp[2 * b:2 * b + 2].reshape(2 * SEQ_P, D)
        m["ck"] = ck[b].reshape(L, 256, 512)
        m["cvv"] = cvv[b].reshape(L, 256, 512)
        maps.append(m)
    return maps


_NC_CACHE = {}


def kernel(**inputs):
    if "nc" not in _NC_CACHE:
        _NC_CACHE["nc"] = build_program()[0]
    nc = _NC_CACHE["nc"]
    maps = make_in_maps(inputs)
    res = run_bass_kernel_spmd(nc, maps, core_ids=list(range(8)))
    R = res.results
    y_prompt = np.stack([R[b // 2]["yp"].reshape(2, SEQ_P, D)[b % 2] for b in range(16)]).astype(np.float32)
    y_sample = np.stack([R[b]["ys"] for b in range(8)]).astype(np.float32)
    nk = np.concatenate([R[b]["nk"] for b in range(8)], 0).reshape(16, L, SEQ_P, NH, HD).astype(np.float32)
    nv = np.concatenate([R[b]["nv"] for b in range(8)], 0).reshape(16, L, SEQ_P, NH, HD).astype(np.float32)
    ns = np.concatenate([R[b]["ns"] for b in range(8)], 0).reshape(16, L, 2, 512).astype(np.float32)
    return (y_prompt, y_sample, nk, nv, ns)
```
